# Optimizing a Trainium2 kernel written in Bass

```python
import math
import jax, jax.numpy as jnp
from jax import lax
import numpy as np

D_MODEL = 1024
BATCH = 2
SEQ = 8192
DEPTH = 1

CHUNK = 64
Q_BLOCK = 128
DA_HEADS = 8
DA_DH = 64
DA_DV = 2 * DA_DH
DA_QK = DA_HEADS * 2 * DA_DH
DA_V = DA_HEADS * DA_DV
CB_HEADS = 8
CB_DH = 64
CB_W = CB_HEADS * CB_DH
CB_LEFT = 8
CB_BAND = (CB_LEFT + 1) * CHUNK
REL_CLIP = 256
REL_SIZE = REL_CLIP + CHUNK
N_BRANCH = 2
D_FF = 2816
CONV_W = 3
ROPE_THETA = 10000.0
LN_EPS = 1e-5
ALPHA = (2 * DEPTH) ** 0.25
BETA = (8 * DEPTH) ** -0.25
IN_SPLITS = [DA_QK, DA_QK, DA_V, CB_W, CB_W, CB_W, N_BRANCH * D_MODEL]

kernel_name = "hybrid_diffattn_chunkband_convffn_deepnorm"


def layer_norm(x, g, b):
    xf = x.astype(jnp.float32)
    mu = jnp.mean(xf, axis=-1, keepdims=True)
    var = jnp.mean(jnp.square(xf - mu), axis=-1, keepdims=True)
    y = (xf - mu) * lax.rsqrt(var + LN_EPS)
    return (y * g.astype(jnp.float32) + b.astype(jnp.float32)).astype(x.dtype)


def rope(t, positions):
    dh = t.shape[-1]
    inv = 1.0 / (ROPE_THETA ** (jnp.arange(0, dh, 2, dtype=jnp.float32) / dh))
    ang = positions.astype(jnp.float32)[..., None] * inv
    cos = jnp.cos(ang)[:, :, None, None, :]
    sin = jnp.sin(ang)[:, :, None, None, :]
    tf = t.astype(jnp.float32)
    t1, t2 = tf[..., : dh // 2], tf[..., dh // 2:]
    out = jnp.concatenate([t1 * cos - t2 * sin, t2 * cos + t1 * sin], axis=-1)
    return out.astype(t.dtype)


def diff_attention(q, k, v, lam, lam_init, subln_g):
    B, S, H, _, dh = q.shape
    nqb = S // Q_BLOCK
    qb = q.reshape(B, nqb, Q_BLOCK, H, 2, dh).transpose(1, 0, 2, 3, 4, 5)
    key_chunk = jnp.arange(S) // CHUNK
    scale = dh ** -0.5

    def block(args):
        qi, i = args
        s = jnp.einsum('bqhtd,bkhtd->bhtqk', qi, k).astype(jnp.float32) * scale
        q_chunk = (i * Q_BLOCK + jnp.arange(Q_BLOCK)) // CHUNK
        mask = key_chunk[None, :] <= q_chunk[:, None]
        s = jnp.where(mask, s, -1e30)
        p = jax.nn.softmax(s, axis=-1)
        w = p[:, :, 0] - lam * p[:, :, 1]
        return jnp.einsum('bhqk,bkhe->bqhe', w.astype(v.dtype), v)

    o = lax.map(block, (qb, jnp.arange(nqb)))
    o = o.transpose(1, 0, 2, 3, 4).reshape(B, S, H, -1)
    of = o.astype(jnp.float32)
    of = of * lax.rsqrt(jnp.mean(jnp.square(of), axis=-1, keepdims=True) + LN_EPS)
    of = of * subln_g.astype(jnp.float32) * (1.0 - lam_init)
    return of.reshape(B, S, -1).astype(v.dtype)


def chunk_band_attention(q, k, v, rel_bias):
    B, S, H, dh = q.shape
    NC = S // CHUNK
    qc = q.reshape(B, NC, CHUNK, H, dh)

    def band(t):
        tc = t.reshape(B, NC, CHUNK, H, dh)
        tp = jnp.pad(tc, ((0, 0), (CB_LEFT, 0), (0, 0), (0, 0), (0, 0)))
        return jnp.concatenate([tp[:, j:j + NC] for j in range(CB_LEFT + 1)], axis=2)

    kb, vb = band(k), band(v)
    slot = jnp.arange(CB_BAND)
    src_chunk = jnp.arange(NC)[:, None] - CB_LEFT + slot[None, :] // CHUNK
    valid = src_chunk >= 0
    rel = slot[None, :] - CB_LEFT * CHUNK - jnp.arange(CHUNK)[:, None]
    idx = jnp.clip(rel, -REL_CLIP, CHUNK - 1) + REL_CLIP
    bias = rel_bias.astype(jnp.float32)[:, idx]
    s = jnp.einsum('bnqhd,bnkhd->bhnqk', qc, kb).astype(jnp.float32) * (dh ** -0.5)
    s = s + bias[None, :, None]
    s = jnp.where(valid[None, None, :, None, :], s, -1e30)
    p = jax.nn.softmax(s, axis=-1)
    o = jnp.einsum('bhnqk,bnkhd->bnqhd', p.astype(v.dtype), vb)
    return o.reshape(B, S, H * dh)


def token_mixer(x, positions, w_in, b_gate, lam, lam_init, subln_g, rel_bias,
                w_proj_a, w_proj_b, w_out):
    B, S, D = x.shape
    proj = x @ w_in
    offs = np.cumsum(IN_SPLITS)[:-1].tolist()
    q_a, k_a, v_a, q_b, k_b, v_b, gates = jnp.split(proj, offs, axis=-1)
    q_a = rope(q_a.reshape(B, S, DA_HEADS, 2, DA_DH), positions)
    k_a = rope(k_a.reshape(B, S, DA_HEADS, 2, DA_DH), positions)
    v_a = v_a.reshape(B, S, DA_HEADS, DA_DV)
    y_a = diff_attention(q_a, k_a, v_a, lam, lam_init, subln_g) @ w_proj_a
    y_b = chunk_band_attention(q_b.reshape(B, S, CB_HEADS, CB_DH),
                               k_b.reshape(B, S, CB_HEADS, CB_DH),
                               v_b.reshape(B, S, CB_HEADS, CB_DH), rel_bias) @ w_proj_b
    g = jax.nn.sigmoid((gates + b_gate).astype(jnp.float32)).astype(x.dtype)
    g = g.reshape(B, S, N_BRANCH, D)
    merged = g[:, :, 0] * y_a + g[:, :, 1] * y_b
    return merged @ w_out


def conv_ffn(x, w_up, conv_w, conv_b, w_down):
    u = x @ w_up
    C = u.shape[-1]
    c = lax.conv_general_dilated(u, conv_w[:, None, :].astype(u.dtype), window_strides=(1,),
                                 padding=[(CONV_W - 1, 0)],
                                 dimension_numbers=('NWC', 'WIO', 'NWC'),
                                 feature_group_count=C) + conv_b
    gate, val = jnp.split(c, 2, axis=-1)
    return (jax.nn.silu(gate) * val) @ w_down


def setup_inputs(seed: int = 0) -> dict:
    key = jax.random.key(seed)
    ks = jax.random.split(key, 32)
    nrm = lambda k, shape, s: jax.random.normal(k, shape, jnp.float32) * s
    D = D_MODEL
    x = jax.random.normal(ks[0], (BATCH, SEQ, D), jnp.float32)
    positions = jnp.broadcast_to(jnp.arange(SEQ, dtype=jnp.int32)[None, :], (BATCH, SEQ))
    sd = D ** -0.5
    w_in = jnp.concatenate([
        nrm(ks[1], (DEPTH, D, DA_QK), sd),
        nrm(ks[2], (DEPTH, D, DA_QK), sd),
        nrm(ks[3], (DEPTH, D, DA_V), BETA * sd),
        nrm(ks[4], (DEPTH, D, CB_W), sd),
        nrm(ks[5], (DEPTH, D, CB_W), sd),
        nrm(ks[6], (DEPTH, D, CB_W), BETA * sd),
        nrm(ks[7], (DEPTH, D, N_BRANCH * D), sd),
    ], axis=-1)
    return {
        "x": x,
        "positions": positions,
        "w_in": w_in,
        "b_gate": nrm(ks[8], (DEPTH, N_BRANCH * D), 0.01),
        "lambda_q1": nrm(ks[9], (DEPTH, DA_DH), 0.1),
        "lambda_k1": nrm(ks[10], (DEPTH, DA_DH), 0.1),
        "lambda_q2": nrm(ks[11], (DEPTH, DA_DH), 0.1),
        "lambda_k2": nrm(ks[12], (DEPTH, DA_DH), 0.1),
        "subln_g": 1.0 + nrm(ks[13], (DEPTH, DA_DV), 0.01),
        "rel_bias": nrm(ks[14], (DEPTH, CB_HEADS, REL_SIZE), 0.1),
        "w_proj_a": nrm(ks[15], (DEPTH, DA_V, D), BETA * DA_V ** -0.5),
        "w_proj_b": nrm(ks[16], (DEPTH, CB_W, D), BETA * CB_W ** -0.5),
        "w_out": nrm(ks[17], (DEPTH, D, D), BETA * sd),
        "ln1_g": 1.0 + nrm(ks[18], (DEPTH, D), 0.01),
        "ln1_b": nrm(ks[19], (DEPTH, D), 0.01),
        "w_up": nrm(ks[20], (DEPTH, D, 2 * D_FF), sd),
        "conv_w": nrm(ks[21], (DEPTH, CONV_W, 2 * D_FF), CONV_W ** -0.5),
        "conv_b": nrm(ks[22], (DEPTH, 2 * D_FF), 0.01),
        "w_down": nrm(ks[23], (DEPTH, D_FF, D), BETA * D_FF ** -0.5),
        "ln2_g": 1.0 + nrm(ks[24], (DEPTH, D), 0.01),
        "ln2_b": nrm(ks[25], (DEPTH, D), 0.01),
    }


def reference(x, positions, w_in, b_gate, lambda_q1, lambda_k1, lambda_q2, lambda_k2,
              subln_g, rel_bias, w_proj_a, w_proj_b, w_out, ln1_g, ln1_b,
              w_up, conv_w, conv_b, w_down, ln2_g, ln2_b):
    h = x
    for l in range(DEPTH):
        lam_init = 0.8 - 0.6 * math.exp(-0.3 * l)
        lam = (jnp.exp(jnp.sum(lambda_q1[l].astype(jnp.float32) * lambda_k1[l].astype(jnp.float32)))
               - jnp.exp(jnp.sum(lambda_q2[l].astype(jnp.float32) * lambda_k2[l].astype(jnp.float32)))
               + lam_init)
        mix = token_mixer(h, positions, w_in[l], b_gate[l], lam, lam_init, subln_g[l], rel_bias[l],
                          w_proj_a[l], w_proj_b[l], w_out[l])
        h = layer_norm(ALPHA * h + mix, ln1_g[l], ln1_b[l])
        h = layer_norm(ALPHA * h + conv_ffn(h, w_up[l], conv_w[l], conv_b[l], w_down[l]), ln2_g[l], ln2_b[l])
    return h
```

```python
import math
from contextlib import ExitStack
import numpy as np
import concourse.bass as bass
import concourse.mybir as mybir
from concourse.bass_utils import run_bass_kernel_spmd

F32 = mybir.dt.float32
BF16 = mybir.dt.bfloat16
I32 = mybir.dt.int32
AF = mybir.ActivationFunctionType
ALU = mybir.AluOpType
AX = mybir.AxisListType

D = 1024
SEQ = 8192
NSLOT = 4
STR = 512
WIN = 640
NLOC = 8192
NTOK = NLOC + NSLOT * WIN
TC = 514
DFF = 2816
NEG = -30000.0
ALPHA = 2.0 ** 0.25
EPS = 1e-5
PI = math.pi
C_INVF, C_SGN, C_HPI, C_EPS, C_BG, C_CW, C_CB, C_VB, C_WVB, C_HV, C_ZERO, NCST = 0, 1, 2, 3, 4, 20, 152, 196, 199, 203, 207, 208

ENGS = ("pe", "act", "dve", "pool", "sp")
N_DMA_SEMS = 20


class Buf:
    __slots__ = ("w", "r")

    def __init__(self):
        self.w = None
        self.r = []


class Op:
    __slots__ = ("eng", "fn", "deps", "sig", "val", "dma", "sem", "prev_val", "idx", "emb")

    def __init__(self, eng, fn, dma, idx):
        self.eng, self.fn, self.dma, self.idx = eng, fn, dma, idx
        self.emb = False
        self.deps, self.sig, self.val, self.sem, self.prev_val = [], False, None, None, 0


class Prog:
    def __init__(self, nc, stack):
        self.nc = nc
        self.q = {e: [] for e in ENGS}
        self.n = 0
        self.bufs = []
        self.sems = {e: stack.enter_context(nc.semaphore("sem_" + e)) for e in ENGS}
        self.dq = {"sp": list(range(0, 10)), "act": list(range(10, 14)), "pool": list(range(14, 20))}
        self.dqk = {"sp": 0, "act": 0, "pool": 0}
        self.dsems = [stack.enter_context(nc.semaphore("dsem%d" % i)) for i in range(N_DMA_SEMS)]
        self.cnt = {e: 0 for e in ENGS}
        self.dcount = [0] * N_DMA_SEMS
        self.dk = 0
        self.waited = {e: {} for e in ENGS}
        self.prev_final = []
        self.bgsem = [stack.enter_context(nc.semaphore("bgsem%d" % i)) for i in range(3)]
        self.bgcount = [0, 0, 0]
        self.bg_need = []

    def buf(self):
        b = Buf()
        self.bufs.append(b)
        return b

    def op(self, eng, fn, reads=(), writes=(), dma=False, emb=False):
        o = Op(eng, fn, dma, self.n)
        o.emb = emb and not dma and eng != "pe"
        self.n += 1
        deps = {}
        for b in reads:
            if b.w is not None:
                deps[id(b.w)] = b.w
        for b in writes:
            if b.w is not None:
                deps[id(b.w)] = b.w
            for r in b.r:
                deps[id(r)] = r
        for d in deps.values():
            if d.eng == eng and not d.dma and not dma:
                if eng == "pe":
                    continue
                if not any(b.w is d for b in reads):
                    continue
            o.deps.append(d)
            d.sig = True
        for b in reads:
            b.r.append(o)
        for b in writes:
            b.w = o
            b.r = []
        self.q[eng].append(o)
        return o

    def bg(self, eng, fn, group):
        o = Op(eng, fn, True, self.n)
        self.n += 1
        self.bgcount[group] += 16
        o.sem = self.bgsem[group]
        o.val = -1
        self.q[eng].append(o)

    def need_bg(self, group):
        self.bg_need.append((self.bgsem[group], self.bgcount[group]))

    def emit_phase(self):
        nc = self.nc
        for e in ENGS:
            ops = self.q[e]
            nd = [o for o in ops if not o.dma]
            if nd:
                nd[-1].sig = True
            for o in ops:
                if not o.dma and o.sig:
                    self.cnt[e] += 1
                    o.sem = self.sems[e]
                    o.val = self.cnt[e]
        allops = sorted([o for e in ENGS for o in self.q[e] if o.dma and o.val != -1], key=lambda o: o.idx)
        for o in allops:
            lst = self.dq[o.eng]
            s = lst[self.dqk[o.eng] % len(lst)]
            self.dqk[o.eng] += 1
            o.sem = self.dsems[s]
            o.prev_val = self.dcount[s]
            self.dcount[s] += 16
            o.val = self.dcount[s]
        prev_final = self.prev_final + self.bg_need
        self.bg_need = []

        def run(ename, eng):
            waited = self.waited[ename]
            for sem, v in prev_final:
                if v > 0 and waited.get(id(sem), 0) < v:
                    eng.wait_ge(sem, v)
                    waited[id(sem)] = v
            for o in self.q[ename]:
                if o.dma and o.prev_val > 0 and waited.get(id(o.sem), 0) < o.prev_val:
                    eng.wait_ge(o.sem, o.prev_val)
                    waited[id(o.sem)] = o.prev_val
                need = {}
                for d in o.deps:
                    if waited.get(id(d.sem), 0) < d.val and need.get(id(d.sem), (None, 0))[1] < d.val:
                        need[id(d.sem)] = (d.sem, d.val)
                need = list(need.values())
                last = need.pop() if (o.emb and need) else None
                for sem, v in need:
                    eng.wait_ge(sem, v)
                    waited[id(sem)] = v
                ins = o.fn(eng)
                if last is not None:
                    ins._wait_ge(last[0], last[1])
                    waited[id(last[0])] = last[1]
                if o.dma:
                    ins.then_inc(o.sem, 16)
                elif o.sig:
                    ins.then_inc(o.sem, 1)

        with nc.Block() as block:
            @block.tensor
            def _(e):
                run("pe", e)

            @block.scalar
            def _(e):
                run("act", e)

            @block.vector
            def _(e):
                run("dve", e)

            @block.gpsimd
            def _(e):
                run("pool", e)

            @block.sync
            def _(e):
                run("sp", e)
        self.prev_final = [(self.sems[e], self.cnt[e]) for e in ENGS] + \
                          [(self.dsems[i], self.dcount[i]) for i in range(N_DMA_SEMS)]
        self.q = {e: [] for e in ENGS}
        for b in self.bufs:
            b.w = None
            b.r = []

    def finish(self):
        nc = self.nc
        pf = self.prev_final
        with nc.Block() as block:
            @block.sync
            def _(e):
                for sem, v in pf:
                    if v > 0:
                        e.wait_ge(sem, v)


def build_program():
    nc = bass.Bass("TRN2", target_bir_lowering=False)
    dt_in = lambda n, s, d=F32: nc.dram_tensor(n, s, d, kind="ExternalInput").ap()
    xT_d = dt_in("xT", [D, NTOK])
    pos_d = dt_in("pos", [1, NTOK], I32)
    xtok_d = dt_in("xtok", [NSLOT * TC, D])
    win_d = dt_in("w_in", [D, 6656])
    wpa_d = dt_in("w_pa", [D, D])
    wpb_d = dt_in("w_pb", [512, D])
    wout_d = dt_in("w_out", [D, D])
    wup_d = dt_in("w_up", [D, 2 * DFF])
    wdn_d = dt_in("w_dn", [DFF, D])
    cst_d = dt_in("cst", [128, NCST])
    bt_d = dt_in("bt", [128, 40 * 128])
    rows_d = dt_in("rows", [9, D])
    ident_d = dt_in("ident", [128, 128])
    masks_d = dt_in("masks", [128, 256])
    out_d = nc.dram_tensor("out", [NSLOT * STR, D], F32, kind="ExternalOutput").ap()
    CS_d = nc.dram_tensor("cs_scr", [128, 2, NTOK], F32).ap()
    H1_d = nc.dram_tensor("h1_scr", [NSLOT * TC, D], F32).ap()
    H1T_d = nc.dram_tensor("h1t_scr", [128, 8, NSLOT * TC], BF16).ap()
    XB_d = nc.dram_tensor("xb_scr", [128, 8, NTOK], BF16).ap()
    winb_d = nc.dram_tensor("winb_scr", [D, 6656], BF16).ap()
    wpab_d = nc.dram_tensor("wpab_scr", [D, D], BF16).ap()
    wpbb_d = nc.dram_tensor("wpbb_scr", [512, D], BF16).ap()
    woutb_d = nc.dram_tensor("woutb_scr", [D, D], BF16).ap()
    wupb_d = nc.dram_tensor("wupb_scr", [D, 2 * DFF], BF16).ap()
    wdnb_d = nc.dram_tensor("wdnb_scr", [DFF, D], BF16).ap()

    xT_r = xT_d.rearrange("(k p) n -> p k n", p=128)

    def bc(a):
        return bass.AP(tensor=a.tensor, offset=a.offset, ap=[[0, 128], [1, a.shape[-1]]])

    with ExitStack() as outer:
        P = Prog(nc, outer)
        cnt = [0]

        def sb(stack, shape, dtype):
            cnt[0] += 1
            return stack.enter_context(nc.sbuf_tensor("t%d" % cnt[0], shape, dtype))

        PS = outer.enter_context(nc.psum_tensor("ps", [128, 8, 512], F32))
        BK = [P.buf() for _ in range(8)]

        def mm(out, lhsT, rhs, start, stop, R, W):
            P.op("pe", lambda e: e.matmul(out, lhsT=lhsT, rhs=rhs, start=start, stop=stop,
                                          skip_group_check=True), R, W)

        def tr(out, in_, ident, R, W):
            P.op("pe", lambda e: e.transpose(out=out, in_=in_, identity=ident), R, W)

        def act(out, in_, func, R, W, bias=None, scale=None, accum=None):
            def f(e):
                kw = {}
                if bias is not None:
                    kw["bias"] = bias
                if scale is not None:
                    kw["scale"] = scale
                if accum is not None:
                    kw["accum_out"] = accum
                return e.activation(out=out, in_=in_, func=func, **kw)
            P.op("act", f, R, W, emb=(accum is None))

        def tt(eng, out, in0, in1, op, R, W):
            P.op(eng, lambda e: e.tensor_tensor(out=out, in0=in0, in1=in1, op=op), R, W, emb=True)

        def ts(eng, out, in0, s1, s2, op0, op1, R, W):
            if op1 is None:
                P.op(eng, lambda e: e.tensor_scalar(out=out, in0=in0, scalar1=s1, scalar2=None, op0=op0), R, W, emb=True)
            else:
                P.op(eng, lambda e: e.tensor_scalar(out=out, in0=in0, scalar1=s1, scalar2=s2, op0=op0, op1=op1), R, W, emb=True)

        def stt(out, in0, scalar, in1, op0, op1, R, W):
            P.op("dve", lambda e: e.scalar_tensor_tensor(out=out, in0=in0, scalar=scalar, in1=in1,
                                                         op0=op0, op1=op1), R, W, emb=True)

        def cp(eng, out, in_, R, W):
            if eng == "act":
                P.op("act", lambda e: e.copy(out=out, in_=in_), R, W, emb=True)
            else:
                P.op(eng, lambda e: e.tensor_copy(out=out, in_=in_), R, W, emb=True)

        def dma(eng, out, in_, R, W):
            P.op(eng, lambda e: e.dma_start(out=out, in_=in_), R, W, dma=True)

        def memset(eng, ap, v, W):
            P.op(eng, lambda e: e.memset(ap, v), (), W)

        identf = sb(outer, [128, 128], F32); b_identf = P.buf()
        identb = sb(outer, [128, 128], BF16); b_identb = P.buf()
        dmaskb = sb(outer, [128, 128], BF16); b_dmaskb = P.buf()
        cst = sb(outer, [128, NCST], F32); b_cst = P.buf()
        Gt = sb(outer, [128, 128], F32); b_G = P.buf()
        lamneg = sb(outer, [128, 1], F32); b_lam = P.buf()

        def cc(c, n=1):
            return cst[:, c:c + n]

        with ExitStack() as ph:
            lv = sb(ph, [128, 4, 64], F32); b_lv = P.buf()
            gsrc = sb(ph, [128, 128], F32); b_gsrc = P.buf()
            tmp = sb(ph, [128, 2, 64], F32); b_tmp = P.buf()
            e12 = sb(ph, [128, 2], F32); b_e12 = P.buf()
            ee = sb(ph, [128, 2], F32); b_ee = P.buf()
            dd = sb(ph, [128, 1], F32); b_dd = P.buf()
            def bgcast(dst, src, rows, grp):
                for r0 in range(0, rows, 128):
                    P.bg("pool", lambda e, r0=r0: e.dma_start(out=dst[r0:r0 + 128, :], in_=src[r0:r0 + 128, :]), grp)
            for r0 in range(0, D, 128):
                P.bg("pool", lambda e, r0=r0: e.dma_start(out=winb_d[r0:r0 + 128, 0:3072], in_=win_d[r0:r0 + 128, 0:3072]), 0)
            dma("sp", identf[:], ident_d, [], [b_identf])
            dma("pool", identb[:], ident_d, [], [b_identb])
            dma("pool", dmaskb[:], masks_d[:, 0:128], [], [b_dmaskb])
            dma("sp", cst[:], cst_d, [], [b_cst])
            for j in range(4):
                dma("sp", lv[:, j, :], bc(rows_d[5 + j:6 + j, 0:64]), [], [b_lv])
            dma("sp", gsrc[:], bc(rows_d[4:5, 0:128]), [], [b_gsrc])
            tt("dve", tmp[:, 0, :], lv[:, 0, :], lv[:, 1, :], ALU.mult, [b_lv], [b_tmp])
            tt("dve", tmp[:, 1, :], lv[:, 2, :], lv[:, 3, :], ALU.mult, [b_lv], [b_tmp])
            P.op("dve", lambda e: e.reduce_sum(out=e12[:, 0:1], in_=tmp[:, 0, :], axis=AX.X), [b_tmp], [b_e12])
            P.op("dve", lambda e: e.reduce_sum(out=e12[:, 1:2], in_=tmp[:, 1, :], axis=AX.X), [b_tmp], [b_e12])
            act(ee[:], e12[:], AF.Exp, [b_e12], [b_ee])
            tt("dve", dd[:], ee[:, 1:2], ee[:, 0:1], ALU.subtract, [b_ee], [b_dd])
            ts("dve", lamneg[:], dd[:], -0.2, None, ALU.add, None, [b_dd], [b_lam])
            ts("dve", Gt[:], gsrc[:], 0.8, None, ALU.mult, None, [b_gsrc], [b_G])
            P.emit_phase()

        b_CS = [P.buf() for _ in range(NTOK // 512)]
        b_XBd = [P.buf() for _ in range(NTOK // 512)]
        with ExitStack() as ph:
            PIb = [sb(ph, [128, 512], I32) for _ in range(2)]; b_PI = [P.buf() for _ in range(2)]
            ANG = [sb(ph, [128, 512], F32) for _ in range(2)]; b_ANG = [P.buf() for _ in range(2)]
            KI = [sb(ph, [128, 512], I32) for _ in range(2)]; b_KI = [P.buf() for _ in range(2)]
            KF = [sb(ph, [128, 512], F32) for _ in range(2)]; b_KF = [P.buf() for _ in range(2)]
            MS = [sb(ph, [128, 2, 512], F32) for _ in range(2)]; b_MS = [P.buf() for _ in range(2)]
            CSB = [sb(ph, [128, 2, 512], F32) for _ in range(2)]; b_CSB = [P.buf() for _ in range(2)]
            XF = [sb(ph, [128, 8, 512], F32) for _ in range(2)]; b_XF = [P.buf() for _ in range(2)]
            XH = [sb(ph, [128, 8, 512], BF16) for _ in range(2)]; b_XH = [P.buf() for _ in range(2)]
            CW1 = 6.28125
            CW2 = 2.0 * PI - 6.28125
            for blk in range(NTOK // 512):
                i2 = blk % 2
                c0 = blk * 512
                dma("sp", XF[i2][:], xT_r[:, :, c0:c0 + 512], [], [b_XF[i2]])
                cp("pool", XH[i2][:, 0:3, :], XF[i2][:, 0:3, :], [b_XF[i2]], [b_XH[i2]])
                cp("act", XH[i2][:, 3:8, :], XF[i2][:, 3:8, :], [b_XF[i2]], [b_XH[i2]])
                dma("act", XB_d[:, :, c0:c0 + 512], XH[i2][:], [b_XH[i2]], [b_XBd[blk]])
                dma("sp", PIb[i2][:], bc(pos_d[0:1, c0:c0 + 512]), [], [b_PI[i2]])
                cp("dve", ANG[i2][:], PIb[i2][:], [b_PI[i2]], [b_ANG[i2]])
                ts("dve", ANG[i2][:], ANG[i2][:], cc(C_INVF), None, ALU.mult, None, [b_ANG[i2], b_cst], [b_ANG[i2]])
                ts("dve", KF[i2][:], ANG[i2][:], 1.0 / (2.0 * PI), None, ALU.mult, None, [b_ANG[i2]], [b_KF[i2]])
                cp("dve", KI[i2][:], KF[i2][:], [b_KF[i2]], [b_KI[i2]])
                cp("dve", KF[i2][:], KI[i2][:], [b_KI[i2]], [b_KF[i2]])
                stt(MS[i2][:, 1, :], KF[i2][:], -CW1, ANG[i2][:], ALU.mult, ALU.add, [b_KF[i2], b_ANG[i2]], [b_MS[i2]])
                stt(MS[i2][:, 1, :], KF[i2][:], -CW2, MS[i2][:, 1, :], ALU.mult, ALU.add, [b_KF[i2], b_MS[i2]], [b_MS[i2]])
                ts("dve", MS[i2][:, 1, :], MS[i2][:, 1, :], -PI, PI, ALU.max, ALU.min, [b_MS[i2]], [b_MS[i2]])
                stt(MS[i2][:, 0, :], MS[i2][:, 1, :], -1.0, MS[i2][:, 1, :], ALU.mult, ALU.max, [b_MS[i2]], [b_MS[i2]])
                act(CSB[i2][:, 0, :], MS[i2][:, 0, :], AF.Sin, [b_MS[i2], b_cst], [b_CSB[i2]], bias=cc(C_HPI), scale=-1.0)
                act(CSB[i2][:, 1, :], MS[i2][:, 1, :], AF.Sin, [b_MS[i2], b_cst], [b_CSB[i2]], bias=cc(C_ZERO), scale=cc(C_SGN))
                dma("act", CS_d[:, :, c0:c0 + 512], CSB[i2][:], [b_CSB[i2]], [b_CS[blk]])
            P.emit_phase()

        P.need_bg(0)
        def rope(t, cs, c_off, n, M1, M2, bM1, bM2, outs, R, W, add_eng="pool"):
            tt("dve", M1[:, :n], t, cs[:, 0, c_off:c_off + n], ALU.mult, R, [bM1])
            tt("dve", M2[0:64, :n], t[64:128], cs[64:128, 1, c_off:c_off + n], ALU.mult, R, [bM2])
            tt("dve", M2[64:128, :n], t[0:64], cs[0:64, 1, c_off:c_off + n], ALU.mult, R, [bM2])
            for (r0, r1, o_ap) in outs:
                tt(add_eng, o_ap, M1[r0:r1, :n], M2[r0:r1, :n], ALU.add, [bM1, bM2], W)

        with ExitStack() as scopeO:
            oT = sb(scopeO, [128, 12, NSLOT * TC], BF16)
            b_oT = [[P.buf() for _ in range(NSLOT)] for _ in range(12)]

            with ExitStack() as scopeA:
                KT = [sb(scopeA, [128, NLOC], BF16) for _ in range(2)]
                b_KT = [[P.buf() for _ in range(16)] for _ in range(2)]
                V = sb(scopeA, [128, 64, 2, 129], BF16)
                b_V = [P.buf() for _ in range(64)]
                Wq = sb(scopeA, [128, 8, 256], BF16); b_Wq = P.buf()
                Wk = sb(scopeA, [128, 8, 256], BF16); b_Wk = P.buf()
                Wv = sb(scopeA, [128, 8, 256], BF16); b_Wv = P.buf()
                XT = [sb(scopeA, [128, 8, 512], BF16) for _ in range(2)]; b_XT = [P.buf() for _ in range(2)]
                CSB = [sb(scopeA, [128, 2, 512], F32) for _ in range(2)]; b_CSB = [P.buf() for _ in range(2)]
                M1 = sb(scopeA, [128, 512], F32); bM1 = P.buf()
                M2 = sb(scopeA, [128, 512], F32); bM2 = P.buf()
                PT = [sb(scopeA, [128, 2, 512], BF16) for _ in range(2)]; b_PT = [P.buf() for _ in range(2)]
                Qz = [[[sb(scopeA, [128, TC], BF16) for _ in range(2)] for _ in range(2)] for _ in range(2)]
                b_Qz = [[P.buf() for _ in range(2)] for _ in range(2)]
                XQ = sb(scopeA, [128, 8, TC], BF16); b_XQ = P.buf()
                CQ = sb(scopeA, [128, 2, TC], F32); b_CQ = P.buf()
                ACCS = sb(scopeA, [128, 3, 387], F32); b_ACCS = P.buf()
                HAS = sb(scopeA, [2, 258], F32); b_HAS = P.buf()
                PH = sb(scopeA, [128, 256], BF16); b_PH = P.buf()
                L2 = sb(scopeA, [128, 2], F32); bL2 = P.buf()
                R2 = sb(scopeA, [128, 2], F32); bR2 = P.buf()
                R1 = sb(scopeA, [128, 1], F32); bR1 = P.buf()
                EA = sb(scopeA, [128, 128], F32); bEA = P.buf()
                EJ = sb(scopeA, [128, 128], F32); bEJ = P.buf()
                EO5 = sb(scopeA, [128, 5, 128], F32); bEO5 = P.buf()
                SS5 = sb(scopeA, [128, 5], F32); bSS5 = P.buf()
                RS5 = sb(scopeA, [128, 5], F32); bRS5 = P.buf()
                ONb = sb(scopeA, [128, 40, 128], BF16); bONb = P.buf()

                def stage1(i, h):
                    srcs = []
                    for t in range(4):
                        a0, a1 = 2 * t, 2 * t + 1
                        srcs.append((ACCS[:, a0 // 3, (a0 % 3) * 129:(a0 % 3) * 129 + 129],
                                     ACCS[:, a1 // 3, (a1 % 3) * 129:(a1 % 3) * 129 + 129], 128, [b_ACCS]))
                    srcs.append((HAS[:, 0:129], HAS[:, 129:258], 2, [b_HAS]))
                    for tl, (a0, a1, nt, Ra) in enumerate(srcs):
                        ts("dve", L2[:nt, 0:1], a0[:, 128:129], 1e-30, None, ALU.max, None, Ra, [bL2])
                        ts("dve", L2[:nt, 1:2], a1[:, 128:129], 1e-30, None, ALU.max, None, Ra, [bL2])
                        P.op("dve", lambda e, nt=nt: e.reciprocal(out=R2[:nt, :], in_=L2[:nt, :]), [bL2], [bR2])
                        ts("dve", R1[:nt, :], R2[:nt, 1:2], lamneg[:nt, 0:1], None, ALU.mult, None, [bR2, b_lam], [bR1])
                        ts("dve", EA[:nt, :], a0[:, 0:128], R2[:nt, 0:1], None, ALU.mult, None, Ra + [bR2], [bEA])
                        stt(EO5[:nt, tl, :], a1[:, 0:128], R1[:nt, 0:1], EA[:nt, :], ALU.mult, ALU.add, Ra + [bR1, bEA], [bEO5])
                        P.op("dve", lambda e, nt=nt, tl=tl: e.scalar_tensor_tensor(
                            out=EJ[:nt, :], in0=EO5[:nt, tl, :], scalar=1.0, in1=EO5[:nt, tl, :], op0=ALU.mult, op1=ALU.mult,
                            accum_out=SS5[:nt, tl:tl + 1]), [bEO5], [bEJ, bSS5])

                def stage2(i, h):
                    act(RS5[:], SS5[:], AF.Ln, [bSS5, b_cst], [bRS5], bias=cc(C_EPS), scale=1.0 / 128)
                    act(RS5[:], RS5[:], AF.Exp, [bRS5], [bRS5], scale=-0.5)
                    for tl in range(5):
                        nt = 128 if tl < 4 else 2
                        stt(ONb[:nt, (2 * i + h) * 5 + tl, :], EO5[:nt, tl, :], RS5[:nt, tl:tl + 1], Gt[:nt, :], ALU.mult, ALU.mult,
                            [bEO5, bRS5, b_G], [bONb])

                memset("pool", V[:, :, :, 128:129], 1.0, b_V)
                memset("pool", SS5[:], 1.0, [bSS5])
                for par in range(2):
                    for h in range(2):
                        for s_ in range(2):
                            memset("pool", Qz[par][h][s_][:], 0.0, [b_Qz[par][h]])

                for hp in range(4):
                    for (Wt, bW, c0) in ((Wq, b_Wq, 0), (Wk, b_Wk, 1024), (Wv, b_Wv, 2048)):
                        dma("sp", Wt[:], winb_d[:, c0 + 256 * hp: c0 + 256 * hp + 256].rearrange("(k p) n -> p k n", p=128),
                            [], [bW])
                    for tb in range(16):
                        x = XT[tb % 2]; bx = b_XT[tb % 2]
                        cs = CSB[tb % 2]; bcs = b_CSB[tb % 2]
                        dma("sp", x[:], XB_d[:, :, tb * 512:(tb + 1) * 512], [b_XBd[tb]], [bx])
                        dma("sp", cs[:], CS_d[:, :, tb * 512:(tb + 1) * 512], [b_CS[tb]], [bcs])
                        for h in range(2):
                            bk = (2 * tb + h) % 2
                            kp = PS[:, bk, :]
                            for k in range(8):
                                mm(kp, Wk[:, k, h * 128:(h + 1) * 128], x[:, k, :], k == 0, k == 7, [b_Wk, bx], [BK[bk]])
                            rope(kp, cs, 0, 512, M1, M2, bM1, bM2,
                                 [(0, 128, KT[h][:, tb * 512:(tb + 1) * 512])], [BK[bk], bcs], [b_KT[h][tb]])
                        for t4 in range(4):
                            bk = 2 + t4 % 2
                            vp = PS[:, bk, 0:256]
                            for k in range(8):
                                mm(vp, x[:, k, t4 * 128:(t4 + 1) * 128], Wv[:, k, :], k == 0, k == 7, [b_Wv, bx], [BK[bk]])
                            cp("act", V[:, tb * 4 + t4, :, 0:128], vp.rearrange("p (a b) -> p a b", a=2), [BK[bk]], [b_V[tb * 4 + t4]])

                    if hp == 0:
                        bgq = []
                    if hp == 0:
                        for r0 in range(0, D, 128):
                            bgq.append((lambda e, r0=r0: e.dma_start(out=winb_d[r0:r0 + 128, 3072:6656], in_=win_d[r0:r0 + 128, 3072:6656]), 1))
                        for (dst, src, rows) in ((wpab_d, wpa_d, D), (wpbb_d, wpb_d, 512), (woutb_d, wout_d, D),
                                                 (wupb_d, wup_d, D), (wdnb_d, wdn_d, DFF)):
                            for r0 in range(0, rows, 128):
                                bgq.append((lambda e, r0=r0, dst=dst, src=src: e.dma_start(out=dst[r0:r0 + 128, :], in_=src[r0:r0 + 128, :]), 1))

                    def issue_bg(n):
                        for _ in range(n):
                            if bgq:
                                fn, grp = bgq.pop(0)
                                P.bg("pool", fn, grp)

                    def load_q(i):
                        own0 = (16 * i + 12) * 128
                        hal0 = NLOC + WIN * i + WIN - 2
                        dma("sp", XQ[:, :, 2:TC], XB_d[:, :, own0:own0 + 512], b_XBd, [b_XQ])
                        dma("sp", XQ[:, :, 0:2], XB_d[:, :, hal0:hal0 + 2], b_XBd, [b_XQ])
                        dma("sp", CQ[:, :, 2:TC], CS_d[:, :, own0:own0 + 512], b_CS, [b_CQ])
                        dma("sp", CQ[:, :, 0:2], CS_d[:, :, hal0:hal0 + 2], b_CS, [b_CQ])

                    def qproj(i, h):
                        par = i % 2
                        for (q0, n) in ((2, 512), (0, 2)):
                            qp = PS[:, 7, 0:n]
                            for k in range(8):
                                mm(qp, Wq[:, k, h * 128:(h + 1) * 128], XQ[:, k, q0:q0 + n], k == 0, k == 7, [b_Wq, b_XQ], [BK[7]])
                            Q0, Q1 = Qz[par][h]
                            rope(qp, CQ, q0, n, M1, M2, bM1, bM2,
                                 [(0, 32, Q0[0:32, q0:q0 + n]), (64, 96, Q0[64:96, q0:q0 + n]),
                                  (32, 64, Q1[32:64, q0:q0 + n]), (96, 128, Q1[96:128, q0:q0 + n])],
                                 [BK[7], b_CQ], [b_Qz[par][h]], add_eng="dve")

                    load_q(0)
                    qproj(0, 0)
                    qproj(0, 1)
                    pending = None
                    for i in range(NSLOT):
                        par = i % 2
                        for h in range(2):
                            Qh = Qz[par][h]
                            bQ = b_Qz[par][h]
                            steps = [(kb, None, -1) for kb in range(16 * i)]
                            for o in range(3):
                                steps += [(16 * i + 4 * o + j, cc(C_VB + o), -1) for j in range(4)]
                            steps += [(16 * i + 12 + j, None, j) for j in range(4)]
                            ns = len(steps)
                            npre = 16 * i + 12

                            def QK(s):
                                kb, _, own = steps[s]
                                c0 = 128 * own if own >= 0 else 0
                                b0 = 2 * (s % 2)
                                for sub in range(2):
                                    mm(PS[:, b0 + sub, c0:512], KT[h][:, kb * 128:(kb + 1) * 128], Qh[sub][:, 2 + c0:TC],
                                       True, own < 0, [b_KT[h][kb // 4], bQ], [BK[b0 + sub]])
                                    if own >= 0:
                                        mm(PS[:, b0 + sub, c0:c0 + 128], identb[:], dmaskb[:], False, True,
                                           [b_identb, b_dmaskb], [BK[b0 + sub]])

                            def EXP(s):
                                kb, bias, own = steps[s]
                                c0 = 128 * own if own >= 0 else 0
                                b0 = 2 * (s % 2)
                                act(PT[s % 2][:, :, c0:512], PS[:, b0:b0 + 2, c0:512], AF.Exp,
                                    [BK[b0], BK[b0 + 1], b_cst], [b_PT[s % 2]], bias=bias, scale=0.125)

                            def PV(s):
                                kb, _, own = steps[s]
                                for t in range(max(own, 0), 4):
                                    for sub in range(2):
                                        a = 2 * t + sub
                                        mm(PS[:, 4 + a // 3, (a % 3) * 129:(a % 3) * 129 + 129],
                                           PT[s % 2][:, sub, t * 128:(t + 1) * 128], V[:, kb, h, :],
                                           s == 0 and a % 3 == 0, own == t, [b_PT[s % 2], b_V[kb]], [BK[4 + a // 3]])

                            def halo_qk(kbs):
                                for kb in kbs:
                                    for sub in range(2):
                                        mm(PS[:, 7, 4 * kb + 2 * sub:4 * kb + 2 * sub + 2], KT[h][:, kb * 128:(kb + 1) * 128],
                                           Qh[sub][:, 0:2], True, True, [b_KT[h][kb // 4], bQ], [BK[7]])

                            def halo_exp():
                                if i > 0:
                                    act(PH[:, 0:64 * i], PS[:, 7, 0:64 * i], AF.Exp, [BK[7]], [b_PH], scale=0.125)
                                for o in range(3):
                                    c0 = 64 * i + 16 * o
                                    act(PH[:, c0:c0 + 16], PS[:, 7, c0:c0 + 16], AF.Exp, [BK[7], b_cst], [b_PH],
                                        bias=cc(C_VB + o), scale=0.125)

                            def halo_pv(kbs):
                                for kb in kbs:
                                    for sub in range(2):
                                        mm(PS[0:2, 7, 240 + sub * 129:240 + sub * 129 + 129], PH[:, 4 * kb + 2 * sub:4 * kb + 2 * sub + 2],
                                           V[:, kb, h, :], kb == 0 and sub == 0, kb == npre - 1, [b_PH, b_V[kb]], [BK[7]])

                            hooks = {}
                            nq = (npre + 3) // 4
                            for j in range(nq):
                                hooks.setdefault(2 + j, []).append(lambda j=j: halo_qk(range(4 * j, min(4 * j + 4, npre))))
                            e_step = 2 + nq + 1
                            hooks.setdefault(e_step, []).append(halo_exp)
                            for j in range(nq):
                                hooks.setdefault(e_step + 2 + j, []).append(lambda j=j: halo_pv(range(4 * j, min(4 * j + 4, npre))))
                            d_step = e_step + 2 + nq
                            hooks.setdefault(d_step, []).append(lambda: cp("dve", HAS[:, :], PS[0:2, 7, 240:498], [BK[7]], [b_HAS]))
                            hooks.setdefault(1, []).append(lambda: issue_bg(3))
                            if pending is not None:
                                hooks.setdefault(min(6, ns - 1), []).append(lambda pnd=pending: stage2(*pnd))
                            if h == 0 and i + 1 < NSLOT:
                                hooks.setdefault(min(d_step + 1, ns - 3), []).append(lambda: (load_q(i + 1), qproj(i + 1, 0)))
                                hooks.setdefault(min(d_step + 3, ns - 1), []).append(lambda: qproj(i + 1, 1))
                            assert d_step < ns - 3 or not (h == 0 and i + 1 < NSLOT), (d_step, ns)

                            QK(0)
                            for s in range(ns):
                                if s + 1 < ns:
                                    QK(s + 1)
                                EXP(s)
                                PV(s)
                                for f in hooks.get(s, []):
                                    f()
                            assert max(hooks) < ns, (max(hooks), ns)
                            cp("dve", ACCS[:, :, :], PS[:, 4:7, 0:387], [BK[4], BK[5], BK[6]], [b_ACCS])
                            stage1(i, h)
                            pending = (i, h)
                    stage2(*pending)
                    if hp == 3:
                        issue_bg(len(bgq))
                    for i in range(NSLOT):
                        for h in range(2):
                            for tl in range(5):
                                nt = 128 if tl < 4 else 2
                                col0 = 2 + 128 * tl if tl < 4 else 0
                                idx = (2 * i + h) * 5 + tl
                                bk = idx % 4
                                pso = PS[:, bk, 0:64].bitcast(BF16)
                                tr(pso[:, 0:nt], ONb[:nt, idx, :], identb[:nt, :nt], [bONb, b_identb], [BK[bk]])
                                cp("dve" if idx % 2 == 0 else "act", oT[:, 2 * hp + h, i * TC + col0: i * TC + col0 + nt], pso[:, 0:nt],
                                   [BK[bk]], [b_oT[2 * hp + h][i]])
                    P.emit_phase()

            P.need_bg(1)
            with ExitStack() as scopeB:
                EB = sb(scopeB, [128, 40 * 128], BF16); b_EB = P.buf()
                if True:
                    btf = sb(scopeB, [128, 40 * 128], F32); b_btf = P.buf()
                    mk = sb(scopeB, [128, 256], F32); b_mk = P.buf()
                    dma("sp", btf[:], bt_d, [], [b_btf])
                    dma("sp", mk[:], masks_d, [], [b_mk])
                    for hd in range(40):
                        dl = hd % 5
                        sl = slice(hd * 128, hd * 128 + 128)
                        if dl == 0 or dl == 4:
                            m_ap = mk[:, 0:128] if dl == 0 else mk[:, 128:256]
                            tt("dve", btf[:, sl], btf[:, sl], m_ap, ALU.add, [b_btf, b_mk], [b_btf])
                    for hq in range(8):
                        act(EB[:, hq * 640:(hq + 1) * 640], btf[:, hq * 640:(hq + 1) * 640], AF.Exp, [b_btf], [b_EB])
                Wqb = sb(scopeB, [128, 8, 512], BF16); b_Wqb = P.buf()
                Wkb = sb(scopeB, [128, 8, 512], BF16); b_Wkb = P.buf()
                Wvb = sb(scopeB, [128, 8, 512], BF16); b_Wvb = P.buf()
                XB = [sb(scopeB, [128, 8, 1152], BF16) for _ in range(2)]; b_XB = [P.buf() for _ in range(2)]
                KbT = sb(scopeB, [128, 4, 1152], BF16); b_KbT = P.buf()
                QbT = sb(scopeB, [128, 4, 2, 640], BF16); b_QbT = P.buf()
                Vb = sb(scopeB, [128, 9, 8, 65], BF16); b_Vb = P.buf()
                PTb = [sb(scopeB, [128, 2, 512], BF16) for _ in range(3)]; b_PTb = [P.buf() for _ in range(3)]
                PTc = [sb(scopeB, [128, 2, 512], BF16) for _ in range(3)]; b_PTc = [P.buf() for _ in range(3)]
                EB3 = EB[:].rearrange("p (h c) -> p h c", h=8)
                L5 = sb(scopeB, [128, 2, 5], F32); bL5 = [P.buf(), P.buf()]
                R5 = sb(scopeB, [128, 2, 5], F32); bR5 = [P.buf(), P.buf()]
                OBn = [sb(scopeB, [128, 5, 128], BF16) for _ in range(4)]; bOBn = [P.buf() for _ in range(4)]
                for (Wt, bW, c0) in ((Wqb, b_Wqb, 3072), (Wkb, b_Wkb, 3584), (Wvb, b_Wvb, 4096)):
                    dma("sp", Wt[:], winb_d[:, c0:c0 + 512].rearrange("(k p) n -> p k n", p=128), [], [bW])
                memset("pool", Vb[:, :, :, 64:65], 1.0, [b_Vb])
                memset("pool", QbT[:], 0.0, [b_QbT])
                cnt_t = [0]
                cnt_p = 0

                def load_xb(i):
                    own0 = (16 * i + 12) * 128
                    w0 = NLOC + WIN * i
                    dma("sp", XB[i % 2][:, :, 0:640], XB_d[:, :, w0:w0 + 640], [], [b_XB[i % 2]])
                    dma("sp", XB[i % 2][:, :, 640:1152], XB_d[:, :, own0:own0 + 512], [], [b_XB[i % 2]])

                load_xb(0)
                for i in range(NSLOT):
                    if i + 1 < NSLOT:
                        load_xb(i + 1)
                    xb = XB[i % 2]; bxb = b_XB[i % 2]
                    for p4 in range(4):
                        for (c0, n) in ((0, 512), (512, 512), (1024, 128)):
                            bk = 4 + (cnt_p % 4); cnt_p += 1
                            for k in range(8):
                                mm(PS[:, bk, 0:n], Wkb[:, k, p4 * 128:(p4 + 1) * 128], xb[:, k, c0:c0 + n], k == 0, k == 7,
                                   [b_Wkb, bxb], [BK[bk]])
                            cp("dve", KbT[:, p4, c0:c0 + n], PS[:, bk, 0:n], [BK[bk]], [b_KbT])
                        for (c0, n) in ((512, 512), (1024, 128)):
                            bk = 4 + (cnt_p % 4); cnt_p += 1
                            for k in range(8):
                                mm(PS[:, bk, 0:n], Wqb[:, k, p4 * 128:(p4 + 1) * 128], xb[:, k, c0:c0 + n], k == 0, k == 7,
                                   [b_Wqb, bxb], [BK[bk]])
                            cp("act", QbT[0:64, p4, 0, c0 - 512:c0 - 512 + n], PS[0:64, bk, 0:n], [BK[bk]], [b_QbT])
                            cp("act", QbT[64:128, p4, 1, c0 - 512:c0 - 512 + n], PS[64:128, bk, 0:n], [BK[bk]], [b_QbT])
                    for blk in range(9):
                        bk = 4 + (cnt_p % 4); cnt_p += 1
                        for k in range(8):
                            mm(PS[:, bk, :], xb[:, k, blk * 128:(blk + 1) * 128], Wvb[:, k, :], k == 0, k == 7, [b_Wvb, bxb], [BK[bk]])
                        cp("act", Vb[:, blk, :, 0:64], PS[:, bk, :].rearrange("p (a b) -> p a b", a=8), [BK[bk]], [b_Vb])
                    csteps = []
                    for p4 in range(4):
                        for kb in range(-5, 4):
                            t0 = max(kb, -1)
                            t1 = min(kb + 4, 3)
                            if t1 - t0 + 1 == 5:
                                csteps.append((p4, kb, t0, t0 + 3, False))
                                csteps.append((p4, kb, t1, t1, False))
                            else:
                                csteps.append((p4, kb, t0, t1, kb == 3))

                    def c_qk(si):
                        p4, kb, t0, t1, last = csteps[si]
                        blk = kb + 5
                        n = (t1 - t0 + 1) * 128
                        sidx = si % 3
                        qc0 = 128 * (t0 + 1)
                        for hh in range(2):
                            mm(PS[:, 2 * sidx + hh, 0:n], KbT[:, p4, blk * 128:(blk + 1) * 128],
                               QbT[:, p4, hh, qc0:qc0 + n], True, True, [b_KbT, b_QbT], [BK[2 * sidx + hh]])

                    def c_rest(si):
                        p4, kb, t0, t1, last = csteps[si]
                        blk = kb + 5
                        n = (t1 - t0 + 1) * 128
                        sidx = si % 3
                        dl0 = t0 - kb
                        S = PS[:, 2 * sidx:2 * sidx + 2, 0:n]
                        bS = [BK[2 * sidx], BK[2 * sidx + 1]]
                        if blk <= 4:
                            act(PTb[sidx][:, :, 0:n], S, AF.Exp, bS + [b_cst], [b_PTb[sidx]], bias=cc(C_WVB + i), scale=0.125)
                        else:
                            act(PTb[sidx][:, :, 0:n], S, AF.Exp, bS, [b_PTb[sidx]], scale=0.125)
                        tt("dve", PTc[sidx][:, :, 0:n], PTb[sidx][:, :, 0:n], EB3[:, 2 * p4:2 * p4 + 2, dl0 * 128:dl0 * 128 + n], ALU.mult,
                           [b_PTb[sidx], b_EB], [b_PTc[sidx]])
                        for hh in range(2):
                            h = 2 * p4 + hh
                            ob = 6 + hh
                            OB = PS[:, ob, 0:325].rearrange("p (a b) -> p a b", b=65)
                            for j in range(t1 - t0 + 1):
                                t = t0 + j
                                mm(OB[:, t + 1, :], PTc[sidx][:, hh, j * 128:(j + 1) * 128], Vb[:, blk, h, :], kb == -5 and j == 0, kb == t,
                                   [b_PTc[sidx], b_Vb], [BK[ob]])
                            if last:
                                r0 = 64 * hh
                                ts("dve", L5[:, hh, :], OB[:, :, 64], 1e-30, None, ALU.max, None, [BK[ob]], [bL5[hh]])
                                P.op("dve", lambda e, hh=hh: e.reciprocal(out=R5[:, hh, :], in_=L5[:, hh, :]), [bL5[hh]], [bR5[hh]])
                                for t in range(-1, 4):
                                    ts("dve", OBn[p4][:, t + 1, r0:r0 + 64], OB[:, t + 1, 0:64], R5[:, hh, t + 1:t + 2], None, ALU.mult, None,
                                       [BK[ob], bR5[hh]], [bOBn[p4]])

                    def c_tr(p4):
                        for t in range(-1, 4):
                            tb_ = cnt_t[0] % 4; cnt_t[0] += 1
                            pso = PS[:, tb_, 0:64].bitcast(BF16)
                            tr(pso, OBn[p4][:, t + 1, :], identb[:], [bOBn[p4], b_identb], [BK[tb_]])
                            if t >= 0:
                                cp("act", oT[:, 8 + p4, i * TC + 2 + 128 * t: i * TC + 2 + 128 * (t + 1)], pso,
                                   [BK[tb_]], [b_oT[8 + p4][i]])
                            else:
                                cp("act", oT[:, 8 + p4, i * TC: i * TC + 2], pso[:, 126:128], [BK[tb_]], [b_oT[8 + p4][i]])

                    c_qk(0)
                    c_qk(1)
                    for si in range(len(csteps)):
                        if si + 2 < len(csteps):
                            c_qk(si + 2)
                        c_rest(si)
                    for p4 in range(4):
                        c_tr(p4)
                P.emit_phase()

            with ExitStack() as scopeT1:
                Wg = sb(scopeT1, [128, 8, 2048], BF16); b_Wg = P.buf()
                Wpa = sb(scopeT1, [128, 8, 1024], BF16); b_Wpa = P.buf()
                Wpb = sb(scopeT1, [128, 4, 1024], BF16); b_Wpb = P.buf()
                Wo = sb(scopeT1, [128, 8, 1024], BF16); b_Wo = P.buf()
                LNP = sb(scopeT1, [128, 2, 1024], F32); b_LNP = P.buf()
                XG = sb(scopeT1, [128, 8, TC], BF16); b_XG = P.buf()
                GG = sb(scopeT1, [128, 2, TC], F32); b_GG = P.buf()
                T1t = sb(scopeT1, [128, TC], F32); b_T1 = P.buf()
                T2t = sb(scopeT1, [128, TC], F32); b_T2 = P.buf()
                MT = sb(scopeT1, [128, 8, TC], BF16); b_MT = [P.buf() for _ in range(8)]
                XTK = [sb(scopeT1, [128, 1024], F32) for _ in range(3)]; b_XTK = [P.buf() for _ in range(3)]
                RR = [sb(scopeT1, [128, 1024], F32) for _ in range(3)]; b_RR = [P.buf() for _ in range(3)]
                H1 = [sb(scopeT1, [128, 1024], F32) for _ in range(3)]; b_H1 = [P.buf() for _ in range(3)]
                H1Ts = sb(scopeT1, [128, 8, TC], BF16); b_H1Ts = P.buf()
                H1b = sb(scopeT1, [128, 1024], BF16); b_H1b = P.buf()
                ST = sb(scopeT1, [128, 2, 6], F32); b_ST = P.buf()
                MV = sb(scopeT1, [128, 2], F32); b_MV = P.buf()
                RSD = sb(scopeT1, [128, 1], F32); b_RSD = P.buf()
                dma("sp", Wg[:], winb_d[:, 4608:6656].rearrange("(k p) n -> p k n", p=128), [], [b_Wg])
                dma("sp", Wpa[:], wpab_d.rearrange("(k p) n -> p k n", p=128), [], [b_Wpa])
                dma("sp", Wpb[:], wpbb_d.rearrange("(k p) n -> p k n", p=128), [], [b_Wpb])
                dma("sp", Wo[:], woutb_d.rearrange("(k p) n -> p k n", p=128), [], [b_Wo])
                for j in range(2):
                    dma("sp", LNP[:, j, :], bc(rows_d[j:j + 1, :]), [], [b_LNP])
                b_H1d = P.buf(); b_H1Td = P.buf()
                HALF = ((0, 257), (257, 257))

                def fm_group(bk0, lhs_list, rhs_fn, R):
                    nk = len(lhs_list)
                    for hf, (c0, n) in enumerate(HALF):
                        for k in range(nk):
                            mm(PS[:, bk0 + hf, 0:n], lhs_list[k], rhs_fn(k, c0, n), k == 0, k == nk - 1, R, [BK[bk0 + hf]])
                    return PS[:, bk0:bk0 + 2, 0:257]

                def v3(t2d):
                    return t2d.rearrange("p (a b) -> p a b", a=2)

                for i in range(NSLOT):
                    own0 = (16 * i + 12) * 128
                    hal0 = NLOC + WIN * i + WIN - 2
                    dma("sp", XG[:, :, 2:TC], XB_d[:, :, own0:own0 + 512], [], [b_XG])
                    dma("sp", XG[:, :, 0:2], XB_d[:, :, hal0:hal0 + 2], [], [b_XG])
                    for m in range(8):
                        for j in range(2):
                            gb = 6 * j
                            g = fm_group(gb, [Wg[:, k, (8 * j + m) * 128:(8 * j + m + 1) * 128] for k in range(8)],
                                         lambda k, c0, n: XG[:, k, c0:c0 + n], [b_Wg, b_XG])
                            act(v3(GG[:, j, :]), g, AF.Sigmoid, [BK[gb], BK[gb + 1], b_cst], [b_GG], bias=cc(C_BG + 8 * j + m), scale=1.0)
                        ya = fm_group(2, [Wpa[:, k, m * 128:(m + 1) * 128] for k in range(8)],
                                      lambda k, c0, n: oT[:, k, i * TC + c0: i * TC + c0 + n], [b_Wpa] + [b_oT[k][i] for k in range(8)])
                        tt("dve", v3(T1t[:]), ya, v3(GG[:, 0, :]), ALU.mult, [BK[2], BK[3], b_GG], [b_T1])
                        yb = fm_group(4, [Wpb[:, k, m * 128:(m + 1) * 128] for k in range(4)],
                                      lambda k, c0, n: oT[:, 8 + k, i * TC + c0: i * TC + c0 + n], [b_Wpb] + [b_oT[8 + k][i] for k in range(4)])
                        tt("dve", v3(T2t[:]), yb, v3(GG[:, 1, :]), ALU.mult, [BK[4], BK[5], b_GG], [b_T2])
                        tt("pool", MT[:, m, :], T1t[:], T2t[:], ALU.add, [b_T1, b_T2], [b_MT[m]])
                    def t_geom(tl):
                        nt = 2 if tl == 0 else 128
                        c0 = 0 if tl == 0 else 2 + 128 * (tl - 1)
                        return nt, c0, i * TC + c0

                    def t_mix(tl):
                        nt, c0, r0 = t_geom(tl)
                        pb = (6, 4, 2)[tl % 3]
                        dma("sp", XTK[tl % 3][:nt, :], xtok_d[r0:r0 + nt, :], [], [b_XTK[tl % 3]])
                        for cg in range(2):
                            for k in range(8):
                                mm(PS[:nt, pb + cg, :], MT[:, k, c0:c0 + nt], Wo[:, k, cg * 512:(cg + 1) * 512], k == 0, k == 7,
                                   [b_MT[k], b_Wo], [BK[pb + cg]])

                    def t_ln(tl):
                        nt, c0, r0 = t_geom(tl)
                        pb = (6, 4, 2)[tl % 3]
                        RRt, bRRt = RR[tl % 3], b_RR[tl % 3]
                        H1t, bH1t = H1[tl % 3], b_H1[tl % 3]
                        stt(RRt[:nt, :], XTK[tl % 3][:nt, :], ALPHA, PS[:nt, pb:pb + 2, :].rearrange("p a b -> p (a b)"), ALU.mult, ALU.add,
                            [b_XTK[tl % 3], BK[pb], BK[pb + 1]], [bRRt])
                        for cg in range(2):
                            P.op("dve", lambda e, cg=cg, nt=nt, RRt=RRt: e.bn_stats(out=ST[:nt, cg, :], in_=RRt[:nt, cg * 512:(cg + 1) * 512]),
                                 [bRRt], [b_ST])
                        P.op("dve", lambda e, nt=nt: e.bn_aggr(out=MV[:nt, :], in_=ST[:nt, :, :].rearrange("p a b -> p (a b)")),
                             [b_ST], [b_MV])
                        act(RSD[:nt, :], MV[:nt, 1:2], AF.Ln, [b_MV, b_cst], [b_RSD], bias=cst[:nt, C_EPS:C_EPS + 1], scale=1.0)
                        act(RSD[:nt, :], RSD[:nt, :], AF.Exp, [b_RSD], [b_RSD], scale=-0.5)
                        ts("dve", RRt[:nt, :], RRt[:nt, :], MV[:nt, 0:1], RSD[:nt, 0:1], ALU.subtract, ALU.mult, [bRRt, b_MV, b_RSD], [bRRt])
                        tt("pool", RRt[:nt, :], RRt[:nt, :], LNP[:nt, 0, :], ALU.mult, [bRRt, b_LNP], [bRRt])
                        tt("pool", H1t[:nt, :], RRt[:nt, :], LNP[:nt, 1, :], ALU.add, [bRRt, b_LNP], [bH1t])
                        dma("pool", H1_d[r0:r0 + nt, :], H1t[:nt, :], [bH1t], [b_H1d])

                    def t_tr(tl):
                        nt, c0, r0 = t_geom(tl)
                        H1t, bH1t = H1[tl % 3], b_H1[tl % 3]
                        cp("dve", H1b[:nt, :], H1t[:nt, :], [bH1t], [b_H1b])
                        for g in range(2):
                            psb = PS[:, g, :].bitcast(BF16).rearrange("p (a b) -> p a b", b=256)
                            for j in range(4):
                                k = 4 * g + j
                                tr(psb[:, j, 0:nt], H1b[:nt, k * 128:(k + 1) * 128], identb[:nt, :nt], [b_H1b, b_identb], [BK[g]])
                            cp("act", H1Ts[:, 4 * g:4 * g + 4, c0:c0 + nt], psb[:, :, 0:nt], [BK[g]], [b_H1Ts])

                    t_mix(0)
                    t_mix(1)
                    for tl in range(5):
                        if tl + 2 < 5:
                            t_mix(tl + 2)
                        t_ln(tl)
                        t_tr(tl)
                    dma("act", H1T_d[:, :, i * TC:(i + 1) * TC], H1Ts[:], [b_H1Ts], [b_H1Td])
                P.emit_phase()

        with ExitStack() as scopeT2:
            Wup = sb(scopeT2, [128, 8, 2 * DFF], BF16); b_Wup = P.buf()
            Wdn = sb(scopeT2, [128, 22, 1024], BF16); b_Wdn = P.buf()
            LNP = sb(scopeT2, [128, 2, 1024], F32); b_LNP = P.buf()
            HT = [sb(scopeT2, [128, 8, TC], BF16) for _ in range(2)]; b_HT = [P.buf(), P.buf()]
            UG = sb(scopeT2, [128, TC], F32); b_UG = P.buf()
            UV = sb(scopeT2, [128, TC], F32); b_UV = P.buf()
            CG = sb(scopeT2, [128, TC], F32); b_CG = P.buf()
            CV = sb(scopeT2, [128, TC], F32); b_CV = P.buf()
            SG = sb(scopeT2, [128, 512], F32); b_SG = P.buf()
            AT = sb(scopeT2, [128, 22, 512], BF16); b_AT = [P.buf() for _ in range(22)]
            H1 = sb(scopeT2, [128, 1024], F32); b_H1 = P.buf()
            RR = sb(scopeT2, [128, 1024], F32); b_RR = P.buf()
            OU = sb(scopeT2, [128, 1024], F32); b_OU = P.buf()
            ST = sb(scopeT2, [128, 2, 6], F32); b_ST = P.buf()
            MV = sb(scopeT2, [128, 2], F32); b_MV = P.buf()
            RSD = sb(scopeT2, [128, 1], F32); b_RSD = P.buf()
            b_out = P.buf()
            b_WupP = [[P.buf() for _ in range(4)] for _ in range(2)]
            PIECES = ((0, 6), (6, 12), (12, 17), (17, 22))
            for pc, (ma, mb) in enumerate(PIECES):
                for gv in range(2):
                    ca, cb = (gv * 22 + ma) * 128, (gv * 22 + mb) * 128
                    dma("sp", Wup[:, :, ca:cb], wupb_d[:, ca:cb].rearrange("(k p) n -> p k n", p=128), [], [b_WupP[gv][pc]])

            def wup_buf(gv, m):
                for pc, (ma, mb) in enumerate(PIECES):
                    if ma <= m < mb:
                        return b_WupP[gv][pc]
            dma("sp", Wdn[:], wdnb_d.rearrange("(k p) n -> p k n", p=128), [], [b_Wdn])
            for j in range(2):
                dma("sp", LNP[:, j, :], bc(rows_d[2 + j:3 + j, :]), [], [b_LNP])
            HALF = ((0, 257), (257, 257))
            A0G = CG; b_A0G = b_CG
            A0V = CV; b_A0V = b_CV
            dma("sp", HT[0][:], H1T_d[:, :, 0:TC], [], [b_HT[0]])
            for i in range(NSLOT):
                HTi = HT[i % 2]; bHTi = b_HT[i % 2]
                for m in range(22):
                    for (gv, U, bU, A0, bA0, bk0) in ((0, UG, b_UG, A0G, b_A0G, 0), (1, UV, b_UV, A0V, b_A0V, 2)):
                        mt = gv * 22 + m
                        for hf, (c0, n) in enumerate(HALF):
                            for k in range(8):
                                mm(PS[:, bk0 + hf, 0:n], Wup[:, k, mt * 128:(mt + 1) * 128], HTi[:, k, c0:c0 + n], k == 0, k == 7,
                                   [wup_buf(gv, m), bHTi], [BK[bk0 + hf]])
                        act(U[:].rearrange("p (a b) -> p a b", a=2), PS[:, bk0:bk0 + 2, 0:257], AF.Copy, [BK[bk0], BK[bk0 + 1]], [bU])
                        act(A0[:].rearrange("p (a b) -> p a b", a=2), PS[:, bk0:bk0 + 2, 0:257], AF.Identity,
                            [BK[bk0], BK[bk0 + 1], b_cst], [bA0], bias=cc(C_CB + mt), scale=cc(C_CW + 2 * 44 + mt))
                        if i == 0:
                            ts("dve", U[:, 0:2], U[:, 0:2], cc(C_HV + i), None, ALU.mult, None, [bU, b_cst], [bU])
                        stt(A0[:, 2:TC], U[:, 1:TC - 1], cc(C_CW + 44 + mt), A0[:, 2:TC], ALU.mult, ALU.add, [bU, bA0, b_cst], [bA0])
                        stt(A0[:, 2:TC], U[:, 0:TC - 2], cc(C_CW + mt), A0[:, 2:TC], ALU.mult, ALU.add, [bU, bA0, b_cst], [bA0])
                    act(SG[:], A0G[:, 2:TC], AF.Silu, [b_A0G], [b_SG])
                    tt("pool", AT[:, m, :], SG[:], A0V[:, 2:TC], ALU.mult, [b_SG, b_A0V], [b_AT[m]])
                if i + 1 < NSLOT:
                    dma("sp", HT[(i + 1) % 2][:], H1T_d[:, :, (i + 1) * TC:(i + 2) * TC], [], [b_HT[(i + 1) % 2]])
                for tl in range(4):
                    r0 = i * TC + 2 + 128 * tl
                    pb = 4 + 2 * (tl % 2)
                    dma("sp", H1[:], H1_d[r0:r0 + 128, :], [], [b_H1])
                    for cg in range(2):
                        for k in range(22):
                            mm(PS[:, pb + cg, :], AT[:, k, tl * 128:(tl + 1) * 128], Wdn[:, k, cg * 512:(cg + 1) * 512], k == 0, k == 21,
                               [b_AT[k], b_Wdn], [BK[pb + cg]])
                    stt(RR[:], H1[:], ALPHA, PS[:, pb:pb + 2, :].rearrange("p a b -> p (a b)"), ALU.mult, ALU.add, [b_H1, BK[pb], BK[pb + 1]], [b_RR])
                    for cg in range(2):
                        P.op("dve", lambda e, cg=cg: e.bn_stats(out=ST[:, cg, :], in_=RR[:, cg * 512:(cg + 1) * 512]), [b_RR], [b_ST])
                    P.op("dve", lambda e: e.bn_aggr(out=MV[:], in_=ST[:].rearrange("p a b -> p (a b)")), [b_ST], [b_MV])
                    act(RSD[:], MV[:, 1:2], AF.Ln, [b_MV, b_cst], [b_RSD], bias=cc(C_EPS), scale=1.0)
                    act(RSD[:], RSD[:], AF.Exp, [b_RSD], [b_RSD], scale=-0.5)
                    ts("dve", RR[:], RR[:], MV[:, 0:1], RSD[:, 0:1], ALU.subtract, ALU.mult, [b_RR, b_MV, b_RSD], [b_RR])
                    tt("pool", RR[:], RR[:], LNP[:, 0, :], ALU.mult, [b_RR, b_LNP], [b_RR])
                    tt("pool", OU[:], RR[:], LNP[:, 1, :], ALU.add, [b_RR, b_LNP], [b_OU])
                    dma("pool", out_d[i * 512 + tl * 128: i * 512 + (tl + 1) * 128, :], OU[:], [b_OU], [b_out])
            P.emit_phase()
        P.finish()
    return nc


_PROG = [None]


def _host_inputs(inputs):
    x = np.asarray(inputs["x"], np.float32)
    pos = np.asarray(inputs["positions"], np.int32)
    w_in = np.asarray(inputs["w_in"], np.float32)[0]
    perm = np.concatenate([np.arange(0, 32), np.arange(64, 96), np.arange(32, 64), np.arange(96, 128)])
    cols = np.arange(6656)
    for base in (0, 1024):
        for h in range(8):
            cols[base + 128 * h: base + 128 * (h + 1)] = base + 128 * h + perm
    w_in_p = np.ascontiguousarray(w_in[:, cols])
    rel = np.asarray(inputs["rel_bias"], np.float32)[0]
    k_ = np.arange(128)[:, None]
    q_ = np.arange(128)[None, :]
    bt = np.zeros((128, 40 * 128), np.float32)
    for h in range(8):
        for dl in range(5):
            idx = np.clip((k_ - q_) - 128 * dl, -256, 63) + 256
            bt[:, (h * 5 + dl) * 128:(h * 5 + dl + 1) * 128] = rel[h][idx]
    rows = np.zeros((9, D), np.float32)
    rows[0] = inputs["ln1_g"][0]; rows[1] = inputs["ln1_b"][0]
    rows[2] = inputs["ln2_g"][0]; rows[3] = inputs["ln2_b"][0]
    rows[4, :128] = inputs["subln_g"][0]
    rows[5, :64] = inputs["lambda_q1"][0]; rows[6, :64] = inputs["lambda_k1"][0]
    rows[7, :64] = inputs["lambda_q2"][0]; rows[8, :64] = inputs["lambda_k2"][0]
    ident = np.eye(128, dtype=np.float32)
    masks = np.zeros((128, 256), np.float32)
    masks[64:128, 0:64] = NEG
    masks[0:64, 128 + 64:256] = NEG
    invf = (np.float32(1.0) / (np.float32(10000.0) ** (np.arange(0, 64, 2, dtype=np.float32) / np.float32(64)))).astype(np.float32)
    p = np.arange(128)
    cst0 = np.zeros((128, NCST), np.float32)
    cst0[:, C_INVF] = invf[p % 32]
    sgn = np.where(p < 64, 1.0, -1.0).astype(np.float32)
    cst0[:, C_SGN] = sgn
    cst0[:, C_HPI] = np.float32(PI / 2)
    cst0[:, C_EPS] = np.float32(EPS)
    cst0[:, C_BG:C_BG + 16] = np.asarray(inputs["b_gate"], np.float32)[0].reshape(16, 128).T
    cw = np.asarray(inputs["conv_w"], np.float32)[0]
    for j in range(3):
        cst0[:, C_CW + 44 * j: C_CW + 44 * (j + 1)] = cw[j].reshape(44, 128).T
    cst0[:, C_CB:C_CB + 44] = np.asarray(inputs["conv_b"], np.float32)[0].reshape(44, 128).T
    shared = {
        "w_in": w_in_p, "w_pa": np.ascontiguousarray(inputs["w_proj_a"][0], np.float32),
        "w_pb": np.ascontiguousarray(inputs["w_proj_b"][0], np.float32),
        "w_out": np.ascontiguousarray(inputs["w_out"][0], np.float32),
        "w_up": np.ascontiguousarray(inputs["w_up"][0], np.float32),
        "w_dn": np.ascontiguousarray(inputs["w_down"][0], np.float32),
        "bt": bt, "rows": rows, "ident": ident, "masks": masks,
    }
    maps = []
    for c in range(8):
        b, r = c // 4, c % 4
        tok = []
        for g in range(4):
            for j in range(4):
                if j != r:
                    tok.append(np.arange(512 * (4 * g + j), 512 * (4 * g + j + 1)))
            tok.append(np.arange(512 * (4 * g + r), 512 * (4 * g + r + 1)))
        for i in range(4):
            t0 = 512 * (4 * i + r)
            tok.append(np.arange(t0 - WIN, t0))
        tok = np.concatenate(tok)
        valid = tok >= 0
        tokc = np.where(valid, tok, 0)
        xT = np.ascontiguousarray(x[b].T[:, tokc])
        xT[:, ~valid] = 0.0
        ps = np.where(valid, pos[b][tokc], 0).astype(np.int32)[None, :]
        xtok = np.zeros((NSLOT * TC, D), np.float32)
        cst = cst0.copy()
        for o in range(3):
            cst[:, C_VB + o] = 0.0 if o < r else NEG
        for i in range(4):
            t0 = 512 * (4 * i + r)
            xtok[i * TC + 2:(i + 1) * TC] = x[b, t0:t0 + 512]
            if t0 >= 2:
                xtok[i * TC:i * TC + 2] = x[b, t0 - 2:t0]
            cst[:, C_WVB + i] = 0.0 if t0 > 0 else NEG
            cst[:, C_HV + i] = 1.0 if t0 > 0 else 0.0
        m = dict(shared)
        m.update({"xT": xT, "pos": ps, "xtok": xtok, "cst": cst})
        maps.append(m)
    return maps


def kernel(**inputs):
    if _PROG[0] is None:
        _PROG[0] = build_program()
    nc = _PROG[0]
    maps = _host_inputs(inputs)
    res = run_bass_kernel_spmd(nc, maps, core_ids=list(range(8)))
    out = np.zeros((2, SEQ, D), np.float32)
    for c in range(8):
        b, r = c // 4, c % 4
        o = np.asarray(res.results[c]["out"], np.float32)
        for i in range(4):
            s = 4 * i + r
            out[b, 512 * s:512 * (s + 1)] = o[512 * i:512 * (i + 1)]
    return out
```

```python
import math
from contextlib import ExitStack
import numpy as np
import concourse.bass as bass
import concourse.mybir as mybir
from concourse.bass_utils import run_bass_kernel_spmd

F32 = mybir.dt.float32
BF16 = mybir.dt.bfloat16
I32 = mybir.dt.int32
AF = mybir.ActivationFunctionType
ALU = mybir.AluOpType
AX = mybir.AxisListType

D = 1024
SEQ = 8192
NSLOT = 4
STR = 512
WIN = 640
NLOC = 8192
NTOK = NLOC + NSLOT * WIN
TC = 514
DFF = 2816
NEG = -30000.0
ALPHA = 2.0 ** 0.25
EPS = 1e-5
PI = math.pi
C_INVF, C_SGN, C_HPI, C_EPS, C_BG, C_CW, C_CB, C_VB, C_WVB, C_HV, C_ZERO, NCST = 0, 1, 2, 3, 4, 20, 152, 196, 199, 203, 207, 208

ENGS = ("pe", "act", "dve", "pool", "sp")
N_DMA_SEMS = 20


class Buf:
    __slots__ = ("w", "r")

    def __init__(self):
        self.w = None
        self.r = []


class Op:
    __slots__ = ("eng", "fn", "deps", "sig", "val", "dma", "sem", "prev_val", "idx", "emb")

    def __init__(self, eng, fn, dma, idx):
        self.eng, self.fn, self.dma, self.idx = eng, fn, dma, idx
        self.emb = False
        self.deps, self.sig, self.val, self.sem, self.prev_val = [], False, None, None, 0


class Prog:
    def __init__(self, nc, stack):
        self.nc = nc
        self.q = {e: [] for e in ENGS}
        self.n = 0
        self.bufs = []
        self.sems = {e: stack.enter_context(nc.semaphore("sem_" + e)) for e in ENGS}
        self.dq = {"sp": list(range(0, 10)), "act": list(range(10, 14)), "pool": list(range(14, 20))}
        self.dqk = {"sp": 0, "act": 0, "pool": 0}
        self.dsems = [stack.enter_context(nc.semaphore("dsem%d" % i)) for i in range(N_DMA_SEMS)]
        self.cnt = {e: 0 for e in ENGS}
        self.dcount = [0] * N_DMA_SEMS
        self.dk = 0
        self.waited = {e: {} for e in ENGS}
        self.prev_final = []
        self.bgsem = [stack.enter_context(nc.semaphore("bgsem%d" % i)) for i in range(3)]
        self.bgcount = [0, 0, 0]
        self.bg_need = []

    def buf(self):
        b = Buf()
        self.bufs.append(b)
        return b

    def op(self, eng, fn, reads=(), writes=(), dma=False, emb=False):
        o = Op(eng, fn, dma, self.n)
        o.emb = emb and not dma and eng != "pe"
        self.n += 1
        deps = {}
        for b in reads:
            if b.w is not None:
                deps[id(b.w)] = b.w
        for b in writes:
            if b.w is not None:
                deps[id(b.w)] = b.w
            for r in b.r:
                deps[id(r)] = r
        for d in deps.values():
            if d.eng == eng and not d.dma and not dma:
                if eng == "pe":
                    continue
                if not any(b.w is d for b in reads):
                    continue
            o.deps.append(d)
            d.sig = True
        for b in reads:
            b.r.append(o)
        for b in writes:
            b.w = o
            b.r = []
        self.q[eng].append(o)
        return o

    def bg(self, eng, fn, group):
        o = Op(eng, fn, True, self.n)
        self.n += 1
        self.bgcount[group] += 16
        o.sem = self.bgsem[group]
        o.val = -1
        self.q[eng].append(o)

    def need_bg(self, group):
        self.bg_need.append((self.bgsem[group], self.bgcount[group]))

    def emit_phase(self):
        nc = self.nc
        for e in ENGS:
            ops = self.q[e]
            nd = [o for o in ops if not o.dma]
            if nd:
                nd[-1].sig = True
            for o in ops:
                if not o.dma and o.sig:
                    self.cnt[e] += 1
                    o.sem = self.sems[e]
                    o.val = self.cnt[e]
        allops = sorted([o for e in ENGS for o in self.q[e] if o.dma and o.val != -1], key=lambda o: o.idx)
        for o in allops:
            lst = self.dq[o.eng]
            s = lst[self.dqk[o.eng] % len(lst)]
            self.dqk[o.eng] += 1
            o.sem = self.dsems[s]
            o.prev_val = self.dcount[s]
            self.dcount[s] += 16
            o.val = self.dcount[s]
        prev_final = self.prev_final + self.bg_need
        self.bg_need = []

        def run(ename, eng):
            waited = self.waited[ename]
            for sem, v in prev_final:
                if v > 0 and waited.get(id(sem), 0) < v:
                    eng.wait_ge(sem, v)
                    waited[id(sem)] = v
            for o in self.q[ename]:
                if o.dma and o.prev_val > 0 and waited.get(id(o.sem), 0) < o.prev_val:
                    eng.wait_ge(o.sem, o.prev_val)
                    waited[id(o.sem)] = o.prev_val
                need = {}
                for d in o.deps:
                    if waited.get(id(d.sem), 0) < d.val and need.get(id(d.sem), (None, 0))[1] < d.val:
                        need[id(d.sem)] = (d.sem, d.val)
                need = list(need.values())
                last = need.pop() if (o.emb and need) else None
                for sem, v in need:
                    eng.wait_ge(sem, v)
                    waited[id(sem)] = v
                ins = o.fn(eng)
                if last is not None:
                    ins._wait_ge(last[0], last[1])
                    waited[id(last[0])] = last[1]
                if o.dma:
                    ins.then_inc(o.sem, 16)
                elif o.sig:
                    ins.then_inc(o.sem, 1)

        with nc.Block() as block:
            @block.tensor
            def _(e):
                run("pe", e)

            @block.scalar
            def _(e):
                run("act", e)

            @block.vector
            def _(e):
                run("dve", e)

            @block.gpsimd
            def _(e):
                run("pool", e)

            @block.sync
            def _(e):
                run("sp", e)
        self.prev_final = [(self.sems[e], self.cnt[e]) for e in ENGS] + \
                          [(self.dsems[i], self.dcount[i]) for i in range(N_DMA_SEMS)]
        self.q = {e: [] for e in ENGS}
        for b in self.bufs:
            b.w = None
            b.r = []

    def finish(self):
        nc = self.nc
        pf = self.prev_final
        with nc.Block() as block:
            @block.sync
            def _(e):
                for sem, v in pf:
                    if v > 0:
                        e.wait_ge(sem, v)


def build_program():
    nc = bass.Bass("TRN2", target_bir_lowering=False)
    dt_in = lambda n, s, d=F32: nc.dram_tensor(n, s, d, kind="ExternalInput").ap()
    xT_d = dt_in("xT", [D, NTOK])
    pos_d = dt_in("pos", [1, NTOK], I32)
    xtok_d = dt_in("xtok", [NSLOT * TC, D])
    win_d = dt_in("w_in", [D, 6656])
    wpa_d = dt_in("w_pa", [D, D])
    wpb_d = dt_in("w_pb", [512, D])
    wout_d = dt_in("w_out", [D, D])
    wup_d = dt_in("w_up", [D, 2 * DFF])
    wdn_d = dt_in("w_dn", [DFF, D])
    cst_d = dt_in("cst", [128, NCST])
    bt_d = dt_in("bt", [128, 40 * 128])
    rows_d = dt_in("rows", [9, D])
    ident_d = dt_in("ident", [128, 128])
    masks_d = dt_in("masks", [128, 256])
    out_d = nc.dram_tensor("out", [NSLOT * STR, D], F32, kind="ExternalOutput").ap()
    CS_d = nc.dram_tensor("cs_scr", [128, 2, NTOK], F32).ap()
    H1_d = nc.dram_tensor("h1_scr", [NSLOT * TC, D], F32).ap()
    H1T_d = nc.dram_tensor("h1t_scr", [128, 8, NSLOT * TC], BF16).ap()
    XB_d = nc.dram_tensor("xb_scr", [128, 8, NTOK], BF16).ap()
    winb_d = nc.dram_tensor("winb_scr", [D, 6656], BF16).ap()
    wpab_d = nc.dram_tensor("wpab_scr", [D, D], BF16).ap()
    wpbb_d = nc.dram_tensor("wpbb_scr", [512, D], BF16).ap()
    woutb_d = nc.dram_tensor("woutb_scr", [D, D], BF16).ap()
    wupb_d = nc.dram_tensor("wupb_scr", [D, 2 * DFF], BF16).ap()
    wdnb_d = nc.dram_tensor("wdnb_scr", [DFF, D], BF16).ap()

    xT_r = xT_d.rearrange("(k p) n -> p k n", p=128)

    def bc(a):
        return bass.AP(tensor=a.tensor, offset=a.offset, ap=[[0, 128], [1, a.shape[-1]]])

    with ExitStack() as outer:
        P = Prog(nc, outer)
        cnt = [0]

        def sb(stack, shape, dtype):
            cnt[0] += 1
            return stack.enter_context(nc.sbuf_tensor("t%d" % cnt[0], shape, dtype))

        PS = outer.enter_context(nc.psum_tensor("ps", [128, 8, 512], F32))
        BK = [P.buf() for _ in range(8)]

        def mm(out, lhsT, rhs, start, stop, R, W):
            P.op("pe", lambda e: e.matmul(out, lhsT=lhsT, rhs=rhs, start=start, stop=stop,
                                          skip_group_check=True), R, W)

        def tr(out, in_, ident, R, W):
            P.op("pe", lambda e: e.transpose(out=out, in_=in_, identity=ident), R, W)

        def act(out, in_, func, R, W, bias=None, scale=None, accum=None):
            def f(e):
                kw = {}
                if bias is not None:
                    kw["bias"] = bias
                if scale is not None:
                    kw["scale"] = scale
                if accum is not None:
                    kw["accum_out"] = accum
                return e.activation(out=out, in_=in_, func=func, **kw)
            P.op("act", f, R, W, emb=(accum is None))

        def tt(eng, out, in0, in1, op, R, W):
            P.op(eng, lambda e: e.tensor_tensor(out=out, in0=in0, in1=in1, op=op), R, W, emb=True)

        def ts(eng, out, in0, s1, s2, op0, op1, R, W):
            if op1 is None:
                P.op(eng, lambda e: e.tensor_scalar(out=out, in0=in0, scalar1=s1, scalar2=None, op0=op0), R, W, emb=True)
            else:
                P.op(eng, lambda e: e.tensor_scalar(out=out, in0=in0, scalar1=s1, scalar2=s2, op0=op0, op1=op1), R, W, emb=True)

        def stt(out, in0, scalar, in1, op0, op1, R, W):
            P.op("dve", lambda e: e.scalar_tensor_tensor(out=out, in0=in0, scalar=scalar, in1=in1,
                                                         op0=op0, op1=op1), R, W, emb=True)

        def cp(eng, out, in_, R, W):
            if eng == "act":
                P.op("act", lambda e: e.copy(out=out, in_=in_), R, W, emb=True)
            else:
                P.op(eng, lambda e: e.tensor_copy(out=out, in_=in_), R, W, emb=True)

        def dma(eng, out, in_, R, W):
            P.op(eng, lambda e: e.dma_start(out=out, in_=in_), R, W, dma=True)

        def memset(eng, ap, v, W):
            P.op(eng, lambda e: e.memset(ap, v), (), W)

        identf = sb(outer, [128, 128], F32); b_identf = P.buf()
        identb = sb(outer, [128, 128], BF16); b_identb = P.buf()
        dmaskb = sb(outer, [128, 128], BF16); b_dmaskb = P.buf()
        cst = sb(outer, [128, NCST], F32); b_cst = P.buf()
        Gt = sb(outer, [128, 128], F32); b_G = P.buf()
        lamneg = sb(outer, [128, 1], F32); b_lam = P.buf()

        def cc(c, n=1):
            return cst[:, c:c + n]

        with ExitStack() as ph:
            lv = sb(ph, [128, 4, 64], F32); b_lv = P.buf()
            gsrc = sb(ph, [128, 128], F32); b_gsrc = P.buf()
            tmp = sb(ph, [128, 2, 64], F32); b_tmp = P.buf()
            e12 = sb(ph, [128, 2], F32); b_e12 = P.buf()
            ee = sb(ph, [128, 2], F32); b_ee = P.buf()
            dd = sb(ph, [128, 1], F32); b_dd = P.buf()
            def bgcast(dst, src, rows, grp):
                for r0 in range(0, rows, 128):
                    P.bg("pool", lambda e, r0=r0: e.dma_start(out=dst[r0:r0 + 128, :], in_=src[r0:r0 + 128, :]), grp)
            for r0 in range(0, D, 128):
                P.bg("pool", lambda e, r0=r0: e.dma_start(out=winb_d[r0:r0 + 128, 0:3072], in_=win_d[r0:r0 + 128, 0:3072]), 0)
            dma("sp", identf[:], ident_d, [], [b_identf])
            dma("pool", identb[:], ident_d, [], [b_identb])
            dma("pool", dmaskb[:], masks_d[:, 0:128], [], [b_dmaskb])
            dma("sp", cst[:], cst_d, [], [b_cst])
            for j in range(4):
                dma("sp", lv[:, j, :], bc(rows_d[5 + j:6 + j, 0:64]), [], [b_lv])
            dma("sp", gsrc[:], bc(rows_d[4:5, 0:128]), [], [b_gsrc])
            tt("dve", tmp[:, 0, :], lv[:, 0, :], lv[:, 1, :], ALU.mult, [b_lv], [b_tmp])
            tt("dve", tmp[:, 1, :], lv[:, 2, :], lv[:, 3, :], ALU.mult, [b_lv], [b_tmp])
            P.op("dve", lambda e: e.reduce_sum(out=e12[:, 0:1], in_=tmp[:, 0, :], axis=AX.X), [b_tmp], [b_e12])
            P.op("dve", lambda e: e.reduce_sum(out=e12[:, 1:2], in_=tmp[:, 1, :], axis=AX.X), [b_tmp], [b_e12])
            act(ee[:], e12[:], AF.Exp, [b_e12], [b_ee])
            tt("dve", dd[:], ee[:, 1:2], ee[:, 0:1], ALU.subtract, [b_ee], [b_dd])
            ts("dve", lamneg[:], dd[:], -0.2, None, ALU.add, None, [b_dd], [b_lam])
            ts("dve", Gt[:], gsrc[:], 0.8, None, ALU.mult, None, [b_gsrc], [b_G])
            P.emit_phase()

        b_CS = [P.buf() for _ in range(NTOK // 512)]
        b_XBd = [P.buf() for _ in range(NTOK // 512)]
        with ExitStack() as ph:
            PIb = [sb(ph, [128, 512], I32) for _ in range(2)]; b_PI = [P.buf() for _ in range(2)]
            ANG = [sb(ph, [128, 512], F32) for _ in range(2)]; b_ANG = [P.buf() for _ in range(2)]
            KI = [sb(ph, [128, 512], I32) for _ in range(2)]; b_KI = [P.buf() for _ in range(2)]
            KF = [sb(ph, [128, 512], F32) for _ in range(2)]; b_KF = [P.buf() for _ in range(2)]
            MS = [sb(ph, [128, 2, 512], F32) for _ in range(2)]; b_MS = [P.buf() for _ in range(2)]
            CSB = [sb(ph, [128, 2, 512], F32) for _ in range(2)]; b_CSB = [P.buf() for _ in range(2)]
            XF = [sb(ph, [128, 8, 512], F32) for _ in range(2)]; b_XF = [P.buf() for _ in range(2)]
            XH = [sb(ph, [128, 8, 512], BF16) for _ in range(2)]; b_XH = [P.buf() for _ in range(2)]
            CW1 = 6.28125
            CW2 = 2.0 * PI - 6.28125
            for blk in range(NTOK // 512):
                i2 = blk % 2
                c0 = blk * 512
                dma("sp", XF[i2][:], xT_r[:, :, c0:c0 + 512], [], [b_XF[i2]])
                cp("pool", XH[i2][:, 0:3, :], XF[i2][:, 0:3, :], [b_XF[i2]], [b_XH[i2]])
                cp("act", XH[i2][:, 3:8, :], XF[i2][:, 3:8, :], [b_XF[i2]], [b_XH[i2]])
                dma("act", XB_d[:, :, c0:c0 + 512], XH[i2][:], [b_XH[i2]], [b_XBd[blk]])
                dma("sp", PIb[i2][:], bc(pos_d[0:1, c0:c0 + 512]), [], [b_PI[i2]])
                cp("dve", ANG[i2][:], PIb[i2][:], [b_PI[i2]], [b_ANG[i2]])
                ts("dve", ANG[i2][:], ANG[i2][:], cc(C_INVF), None, ALU.mult, None, [b_ANG[i2], b_cst], [b_ANG[i2]])
                ts("dve", KF[i2][:], ANG[i2][:], 1.0 / (2.0 * PI), None, ALU.mult, None, [b_ANG[i2]], [b_KF[i2]])
                cp("dve", KI[i2][:], KF[i2][:], [b_KF[i2]], [b_KI[i2]])
                cp("dve", KF[i2][:], KI[i2][:], [b_KI[i2]], [b_KF[i2]])
                stt(MS[i2][:, 1, :], KF[i2][:], -CW1, ANG[i2][:], ALU.mult, ALU.add, [b_KF[i2], b_ANG[i2]], [b_MS[i2]])
                stt(MS[i2][:, 1, :], KF[i2][:], -CW2, MS[i2][:, 1, :], ALU.mult, ALU.add, [b_KF[i2], b_MS[i2]], [b_MS[i2]])
                ts("dve", MS[i2][:, 1, :], MS[i2][:, 1, :], -PI, PI, ALU.max, ALU.min, [b_MS[i2]], [b_MS[i2]])
                stt(MS[i2][:, 0, :], MS[i2][:, 1, :], -1.0, MS[i2][:, 1, :], ALU.mult, ALU.max, [b_MS[i2]], [b_MS[i2]])
                act(CSB[i2][:, 0, :], MS[i2][:, 0, :], AF.Sin, [b_MS[i2], b_cst], [b_CSB[i2]], bias=cc(C_HPI), scale=-1.0)
                act(CSB[i2][:, 1, :], MS[i2][:, 1, :], AF.Sin, [b_MS[i2], b_cst], [b_CSB[i2]], bias=cc(C_ZERO), scale=cc(C_SGN))
                dma("act", CS_d[:, :, c0:c0 + 512], CSB[i2][:], [b_CSB[i2]], [b_CS[blk]])
            P.emit_phase()

        P.need_bg(0)
        def rope(t, cs, c_off, n, M1, M2, bM1, bM2, outs, R, W, add_eng="pool"):
            tt("dve", M1[:, :n], t, cs[:, 0, c_off:c_off + n], ALU.mult, R, [bM1])
            tt("dve", M2[0:64, :n], t[64:128], cs[64:128, 1, c_off:c_off + n], ALU.mult, R, [bM2])
            tt("dve", M2[64:128, :n], t[0:64], cs[0:64, 1, c_off:c_off + n], ALU.mult, R, [bM2])
            for (r0, r1, o_ap) in outs:
                tt(add_eng, o_ap, M1[r0:r1, :n], M2[r0:r1, :n], ALU.add, [bM1, bM2], W)

        with ExitStack() as scopeO:
            oT = sb(scopeO, [128, 12, NSLOT * TC], BF16)
            b_oT = [[P.buf() for _ in range(NSLOT)] for _ in range(12)]

            with ExitStack() as scopeA:
                KT = [sb(scopeA, [128, NLOC], BF16) for _ in range(2)]
                b_KT = [[P.buf() for _ in range(16)] for _ in range(2)]
                V = sb(scopeA, [128, 64, 2, 129], BF16)
                b_V = [P.buf() for _ in range(64)]
                Wq = sb(scopeA, [128, 8, 256], BF16); b_Wq = P.buf()
                Wk = sb(scopeA, [128, 8, 256], BF16); b_Wk = P.buf()
                Wv = sb(scopeA, [128, 8, 256], BF16); b_Wv = P.buf()
                XT = [sb(scopeA, [128, 8, 512], BF16) for _ in range(2)]; b_XT = [P.buf() for _ in range(2)]
                CSB = [sb(scopeA, [128, 2, 512], F32) for _ in range(2)]; b_CSB = [P.buf() for _ in range(2)]
                M1 = sb(scopeA, [128, 512], F32); bM1 = P.buf()
                M2 = sb(scopeA, [128, 512], F32); bM2 = P.buf()
                PT = [sb(scopeA, [128, 2, 512], BF16) for _ in range(2)]; b_PT = [P.buf() for _ in range(2)]
                Qz = [[[sb(scopeA, [128, TC], BF16) for _ in range(2)] for _ in range(2)] for _ in range(2)]
                b_Qz = [[P.buf() for _ in range(2)] for _ in range(2)]
                XQ = sb(scopeA, [128, 8, TC], BF16); b_XQ = P.buf()
                CQ = sb(scopeA, [128, 2, TC], F32); b_CQ = P.buf()
                ACCS = sb(scopeA, [128, 3, 387], F32); b_ACCS = P.buf()
                HAS = sb(scopeA, [2, 258], F32); b_HAS = P.buf()
                PH = sb(scopeA, [128, 256], BF16); b_PH = P.buf()
                L2 = sb(scopeA, [128, 2], F32); bL2 = P.buf()
                R2 = sb(scopeA, [128, 2], F32); bR2 = P.buf()
                R1 = sb(scopeA, [128, 1], F32); bR1 = P.buf()
                EA = sb(scopeA, [128, 128], F32); bEA = P.buf()
                EJ = sb(scopeA, [128, 128], F32); bEJ = P.buf()
                EO5 = sb(scopeA, [128, 5, 128], F32); bEO5 = P.buf()
                SS5 = sb(scopeA, [128, 5], F32); bSS5 = P.buf()
                RS5 = sb(scopeA, [128, 5], F32); bRS5 = P.buf()
                ONb = sb(scopeA, [128, 40, 128], BF16); bONb = P.buf()

                def stage1(i, h):
                    srcs = []
                    for t in range(4):
                        a0, a1 = 2 * t, 2 * t + 1
                        srcs.append((ACCS[:, a0 // 3, (a0 % 3) * 129:(a0 % 3) * 129 + 129],
                                     ACCS[:, a1 // 3, (a1 % 3) * 129:(a1 % 3) * 129 + 129], 128, [b_ACCS]))
                    srcs.append((HAS[:, 0:129], HAS[:, 129:258], 2, [b_HAS]))
                    for tl, (a0, a1, nt, Ra) in enumerate(srcs):
                        ts("dve", L2[:nt, 0:1], a0[:, 128:129], 1e-30, None, ALU.max, None, Ra, [bL2])
                        ts("dve", L2[:nt, 1:2], a1[:, 128:129], 1e-30, None, ALU.max, None, Ra, [bL2])
                        P.op("dve", lambda e, nt=nt: e.reciprocal(out=R2[:nt, :], in_=L2[:nt, :]), [bL2], [bR2])
                        ts("dve", R1[:nt, :], R2[:nt, 1:2], lamneg[:nt, 0:1], None, ALU.mult, None, [bR2, b_lam], [bR1])
                        ts("dve", EA[:nt, :], a0[:, 0:128], R2[:nt, 0:1], None, ALU.mult, None, Ra + [bR2], [bEA])
                        stt(EO5[:nt, tl, :], a1[:, 0:128], R1[:nt, 0:1], EA[:nt, :], ALU.mult, ALU.add, Ra + [bR1, bEA], [bEO5])
                        P.op("dve", lambda e, nt=nt, tl=tl: e.scalar_tensor_tensor(
                            out=EJ[:nt, :], in0=EO5[:nt, tl, :], scalar=1.0, in1=EO5[:nt, tl, :], op0=ALU.mult, op1=ALU.mult,
                            accum_out=SS5[:nt, tl:tl + 1]), [bEO5], [bEJ, bSS5])

                def stage2(i, h):
                    act(RS5[:], SS5[:], AF.Ln, [bSS5, b_cst], [bRS5], bias=cc(C_EPS), scale=1.0 / 128)
                    act(RS5[:], RS5[:], AF.Exp, [bRS5], [bRS5], scale=-0.5)
                    for tl in range(5):
                        nt = 128 if tl < 4 else 2
                        stt(ONb[:nt, (2 * i + h) * 5 + tl, :], EO5[:nt, tl, :], RS5[:nt, tl:tl + 1], Gt[:nt, :], ALU.mult, ALU.mult,
                            [bEO5, bRS5, b_G], [bONb])

                memset("pool", V[:, :, :, 128:129], 1.0, b_V)
                memset("pool", SS5[:], 1.0, [bSS5])
                for par in range(2):
                    for h in range(2):
                        for s_ in range(2):
                            memset("pool", Qz[par][h][s_][:], 0.0, [b_Qz[par][h]])

                for hp in range(4):
                    for (Wt, bW, c0) in ((Wq, b_Wq, 0), (Wk, b_Wk, 1024), (Wv, b_Wv, 2048)):
                        dma("sp", Wt[:], winb_d[:, c0 + 256 * hp: c0 + 256 * hp + 256].rearrange("(k p) n -> p k n", p=128),
                            [], [bW])
                    for tb in range(16):
                        x = XT[tb % 2]; bx = b_XT[tb % 2]
                        cs = CSB[tb % 2]; bcs = b_CSB[tb % 2]
                        dma("sp", x[:], XB_d[:, :, tb * 512:(tb + 1) * 512], [b_XBd[tb]], [bx])
                        dma("sp", cs[:], CS_d[:, :, tb * 512:(tb + 1) * 512], [b_CS[tb]], [bcs])
                        for h in range(2):
                            bk = (2 * tb + h) % 2
                            kp = PS[:, bk, :]
                            for k in range(8):
                                mm(kp, Wk[:, k, h * 128:(h + 1) * 128], x[:, k, :], k == 0, k == 7, [b_Wk, bx], [BK[bk]])
                            rope(kp, cs, 0, 512, M1, M2, bM1, bM2,
                                 [(0, 128, KT[h][:, tb * 512:(tb + 1) * 512])], [BK[bk], bcs], [b_KT[h][tb]])
                        for t4 in range(4):
                            bk = 4 + t4 % 4
                            vp = PS[:, bk, 0:256]
                            for k in range(8):
                                mm(vp, x[:, k, t4 * 128:(t4 + 1) * 128], Wv[:, k, :], k == 0, k == 7, [b_Wv, bx], [BK[bk]])
                            cp("act", V[:, tb * 4 + t4, :, 0:128], vp.rearrange("p (a b) -> p a b", a=2), [BK[bk]], [b_V[tb * 4 + t4]])

                    if hp == 0:
                        bgq = []
                    if hp == 0:
                        for r0 in range(0, D, 128):
                            bgq.append((lambda e, r0=r0: e.dma_start(out=winb_d[r0:r0 + 128, 3072:6656], in_=win_d[r0:r0 + 128, 3072:6656]), 1))
                        for (dst, src, rows) in ((wpab_d, wpa_d, D), (wpbb_d, wpb_d, 512), (woutb_d, wout_d, D),
                                                 (wupb_d, wup_d, D), (wdnb_d, wdn_d, DFF)):
                            for r0 in range(0, rows, 128):
                                bgq.append((lambda e, r0=r0, dst=dst, src=src: e.dma_start(out=dst[r0:r0 + 128, :], in_=src[r0:r0 + 128, :]), 1))

                    def issue_bg(n):
                        for _ in range(n):
                            if bgq:
                                fn, grp = bgq.pop(0)
                                P.bg("pool", fn, grp)

                    def load_q(i):
                        own0 = (16 * i + 12) * 128
                        hal0 = NLOC + WIN * i + WIN - 2
                        dma("sp", XQ[:, :, 2:TC], XB_d[:, :, own0:own0 + 512], b_XBd, [b_XQ])
                        dma("sp", XQ[:, :, 0:2], XB_d[:, :, hal0:hal0 + 2], b_XBd, [b_XQ])
                        dma("sp", CQ[:, :, 2:TC], CS_d[:, :, own0:own0 + 512], b_CS, [b_CQ])
                        dma("sp", CQ[:, :, 0:2], CS_d[:, :, hal0:hal0 + 2], b_CS, [b_CQ])

                    def qproj(i, h):
                        par = i % 2
                        for (q0, n) in ((2, 512), (0, 2)):
                            qp = PS[:, 7, 0:n]
                            for k in range(8):
                                mm(qp, Wq[:, k, h * 128:(h + 1) * 128], XQ[:, k, q0:q0 + n], k == 0, k == 7, [b_Wq, b_XQ], [BK[7]])
                            Q0, Q1 = Qz[par][h]
                            rope(qp, CQ, q0, n, M1, M2, bM1, bM2,
                                 [(0, 32, Q0[0:32, q0:q0 + n]), (64, 96, Q0[64:96, q0:q0 + n]),
                                  (32, 64, Q1[32:64, q0:q0 + n]), (96, 128, Q1[96:128, q0:q0 + n])],
                                 [BK[7], b_CQ], [b_Qz[par][h]], add_eng="dve")

                    load_q(0)
                    qproj(0, 0)
                    qproj(0, 1)
                    pending = None
                    for i in range(NSLOT):
                        par = i % 2
                        for h in range(2):
                            Qh = Qz[par][h]
                            bQ = b_Qz[par][h]
                            steps = [(kb, None, -1) for kb in range(16 * i)]
                            for o in range(3):
                                steps += [(16 * i + 4 * o + j, cc(C_VB + o), -1) for j in range(4)]
                            steps += [(16 * i + 12 + j, None, j) for j in range(4)]
                            ns = len(steps)
                            npre = 16 * i + 12

                            def QK(s):
                                kb, _, own = steps[s]
                                c0 = 128 * own if own >= 0 else 0
                                b0 = 2 * (s % 2)
                                for sub in range(2):
                                    mm(PS[:, b0 + sub, c0:512], KT[h][:, kb * 128:(kb + 1) * 128], Qh[sub][:, 2 + c0:TC],
                                       True, own < 0, [b_KT[h][kb // 4], bQ], [BK[b0 + sub]])
                                    if own >= 0:
                                        mm(PS[:, b0 + sub, c0:c0 + 128], identb[:], dmaskb[:], False, True,
                                           [b_identb, b_dmaskb], [BK[b0 + sub]])

                            def EXP(s):
                                kb, bias, own = steps[s]
                                c0 = 128 * own if own >= 0 else 0
                                b0 = 2 * (s % 2)
                                act(PT[s % 2][:, :, c0:512], PS[:, b0:b0 + 2, c0:512], AF.Exp,
                                    [BK[b0], BK[b0 + 1], b_cst], [b_PT[s % 2]], bias=bias, scale=0.125)

                            def PV(s):
                                kb, _, own = steps[s]
                                for t in range(max(own, 0), 4):
                                    for sub in range(2):
                                        a = 2 * t + sub
                                        mm(PS[:, 4 + a // 3, (a % 3) * 129:(a % 3) * 129 + 129],
                                           PT[s % 2][:, sub, t * 128:(t + 1) * 128], V[:, kb, h, :],
                                           s == 0 and a % 3 == 0, own == t, [b_PT[s % 2], b_V[kb]], [BK[4 + a // 3]])

                            def halo_qk(kbs):
                                for kb in kbs:
                                    for sub in range(2):
                                        mm(PS[:, 7, 4 * kb + 2 * sub:4 * kb + 2 * sub + 2], KT[h][:, kb * 128:(kb + 1) * 128],
                                           Qh[sub][:, 0:2], True, True, [b_KT[h][kb // 4], bQ], [BK[7]])

                            def halo_exp():
                                if i > 0:
                                    act(PH[:, 0:64 * i], PS[:, 7, 0:64 * i], AF.Exp, [BK[7]], [b_PH], scale=0.125)
                                for o in range(3):
                                    c0 = 64 * i + 16 * o
                                    act(PH[:, c0:c0 + 16], PS[:, 7, c0:c0 + 16], AF.Exp, [BK[7], b_cst], [b_PH],
                                        bias=cc(C_VB + o), scale=0.125)

                            def halo_pv(kbs):
                                for kb in kbs:
                                    for sub in range(2):
                                        mm(PS[0:2, 7, 240 + sub * 129:240 + sub * 129 + 129], PH[:, 4 * kb + 2 * sub:4 * kb + 2 * sub + 2],
                                           V[:, kb, h, :], kb == 0 and sub == 0, kb == npre - 1, [b_PH, b_V[kb]], [BK[7]])

                            hooks = {}
                            nq = (npre + 3) // 4
                            for j in range(nq):
                                hooks.setdefault(2 + j, []).append(lambda j=j: halo_qk(range(4 * j, min(4 * j + 4, npre))))
                            e_step = 2 + nq + 1
                            hooks.setdefault(e_step, []).append(halo_exp)
                            for j in range(nq):
                                hooks.setdefault(e_step + 2 + j, []).append(lambda j=j: halo_pv(range(4 * j, min(4 * j + 4, npre))))
                            d_step = e_step + 2 + nq
                            hooks.setdefault(d_step, []).append(lambda: cp("dve", HAS[:, :], PS[0:2, 7, 240:498], [BK[7]], [b_HAS]))
                            hooks.setdefault(1, []).append(lambda: issue_bg(3))
                            if pending is not None:
                                hooks.setdefault(min(6, ns - 1), []).append(lambda pnd=pending: stage2(*pnd))
                            if h == 0 and i + 1 < NSLOT:
                                hooks.setdefault(min(d_step + 1, ns - 3), []).append(lambda: (load_q(i + 1), qproj(i + 1, 0)))
                                hooks.setdefault(min(d_step + 3, ns - 1), []).append(lambda: qproj(i + 1, 1))
                            assert d_step < ns - 3 or not (h == 0 and i + 1 < NSLOT), (d_step, ns)

                            QK(0)
                            for s in range(ns):
                                if s + 1 < ns:
                                    QK(s + 1)
                                EXP(s)
                                PV(s)
                                for f in hooks.get(s, []):
                                    f()
                            assert max(hooks) < ns, (max(hooks), ns)
                            cp("dve", ACCS[:, :, :], PS[:, 4:7, 0:387], [BK[4], BK[5], BK[6]], [b_ACCS])
                            stage1(i, h)
                            pending = (i, h)
                    stage2(*pending)
                    if hp == 3:
                        issue_bg(len(bgq))
                    for i in range(NSLOT):
                        for h in range(2):
                            for tl in range(5):
                                nt = 128 if tl < 4 else 2
                                col0 = 2 + 128 * tl if tl < 4 else 0
                                idx = (2 * i + h) * 5 + tl
                                bk = idx % 4
                                pso = PS[:, bk, 0:64].bitcast(BF16)
                                tr(pso[:, 0:nt], ONb[:nt, idx, :], identb[:nt, :nt], [bONb, b_identb], [BK[bk]])
                                cp("dve" if idx % 2 == 0 else "act", oT[:, 2 * hp + h, i * TC + col0: i * TC + col0 + nt], pso[:, 0:nt],
                                   [BK[bk]], [b_oT[2 * hp + h][i]])
                    P.emit_phase()

            P.need_bg(1)
            with ExitStack() as scopeB:
                EB = sb(scopeB, [128, 40 * 128], BF16); b_EB = P.buf()
                if True:
                    btf = sb(scopeB, [128, 40 * 128], F32); b_btf = P.buf()
                    mk = sb(scopeB, [128, 256], F32); b_mk = P.buf()
                    dma("sp", btf[:], bt_d, [], [b_btf])
                    dma("sp", mk[:], masks_d, [], [b_mk])
                    for hd in range(40):
                        dl = hd % 5
                        sl = slice(hd * 128, hd * 128 + 128)
                        if dl == 0 or dl == 4:
                            m_ap = mk[:, 0:128] if dl == 0 else mk[:, 128:256]
                            tt("dve", btf[:, sl], btf[:, sl], m_ap, ALU.add, [b_btf, b_mk], [b_btf])
                    for hq in range(8):
                        act(EB[:, hq * 640:(hq + 1) * 640], btf[:, hq * 640:(hq + 1) * 640], AF.Exp, [b_btf], [b_EB])
                Wqb = sb(scopeB, [128, 8, 512], BF16); b_Wqb = P.buf()
                Wkb = sb(scopeB, [128, 8, 512], BF16); b_Wkb = P.buf()
                Wvb = sb(scopeB, [128, 8, 512], BF16); b_Wvb = P.buf()
                XB = [sb(scopeB, [128, 8, 1152], BF16) for _ in range(2)]; b_XB = [P.buf() for _ in range(2)]
                KbT = sb(scopeB, [128, 4, 1152], BF16); b_KbT = P.buf()
                QbT = sb(scopeB, [128, 4, 2, 640], BF16); b_QbT = P.buf()
                Vb = sb(scopeB, [128, 9, 8, 65], BF16); b_Vb = P.buf()
                PTb = [sb(scopeB, [128, 2, 512], BF16) for _ in range(3)]; b_PTb = [P.buf() for _ in range(3)]
                PTc = [sb(scopeB, [128, 2, 512], BF16) for _ in range(3)]; b_PTc = [P.buf() for _ in range(3)]
                EB3 = EB[:].rearrange("p (h c) -> p h c", h=8)
                L5 = sb(scopeB, [128, 2, 5], F32); bL5 = [P.buf(), P.buf()]
                R5 = sb(scopeB, [128, 2, 5], F32); bR5 = [P.buf(), P.buf()]
                OBn = [sb(scopeB, [128, 5, 128], BF16) for _ in range(4)]; bOBn = [P.buf() for _ in range(4)]
                for (Wt, bW, c0) in ((Wqb, b_Wqb, 3072), (Wkb, b_Wkb, 3584), (Wvb, b_Wvb, 4096)):
                    dma("sp", Wt[:], winb_d[:, c0:c0 + 512].rearrange("(k p) n -> p k n", p=128), [], [bW])
                memset("pool", Vb[:, :, :, 64:65], 1.0, [b_Vb])
                memset("pool", QbT[:], 0.0, [b_QbT])
                cnt_t = [0]
                cnt_p = 0

                def load_xb(i):
                    own0 = (16 * i + 12) * 128
                    w0 = NLOC + WIN * i
                    dma("sp", XB[i % 2][:, :, 0:640], XB_d[:, :, w0:w0 + 640], [], [b_XB[i % 2]])
                    dma("sp", XB[i % 2][:, :, 640:1152], XB_d[:, :, own0:own0 + 512], [], [b_XB[i % 2]])

                load_xb(0)
                for i in range(NSLOT):
                    if i + 1 < NSLOT:
                        load_xb(i + 1)
                    xb = XB[i % 2]; bxb = b_XB[i % 2]
                    for p4 in range(4):
                        for (c0, n) in ((0, 512), (512, 512), (1024, 128)):
                            bk = 4 + (cnt_p % 4); cnt_p += 1
                            for k in range(8):
                                mm(PS[:, bk, 0:n], Wkb[:, k, p4 * 128:(p4 + 1) * 128], xb[:, k, c0:c0 + n], k == 0, k == 7,
                                   [b_Wkb, bxb], [BK[bk]])
                            cp("dve", KbT[:, p4, c0:c0 + n], PS[:, bk, 0:n], [BK[bk]], [b_KbT])
                        for (c0, n) in ((512, 512), (1024, 128)):
                            bk = 4 + (cnt_p % 4); cnt_p += 1
                            for k in range(8):
                                mm(PS[:, bk, 0:n], Wqb[:, k, p4 * 128:(p4 + 1) * 128], xb[:, k, c0:c0 + n], k == 0, k == 7,
                                   [b_Wqb, bxb], [BK[bk]])
                            cp("act", QbT[0:64, p4, 0, c0 - 512:c0 - 512 + n], PS[0:64, bk, 0:n], [BK[bk]], [b_QbT])
                            cp("act", QbT[64:128, p4, 1, c0 - 512:c0 - 512 + n], PS[64:128, bk, 0:n], [BK[bk]], [b_QbT])
                    for blk in range(9):
                        bk = 4 + (cnt_p % 4); cnt_p += 1
                        for k in range(8):
                            mm(PS[:, bk, :], xb[:, k, blk * 128:(blk + 1) * 128], Wvb[:, k, :], k == 0, k == 7, [b_Wvb, bxb], [BK[bk]])
                        cp("act", Vb[:, blk, :, 0:64], PS[:, bk, :].rearrange("p (a b) -> p a b", a=8), [BK[bk]], [b_Vb])
                    csteps = []
                    for p4 in range(4):
                        for kb in range(-5, 4):
                            t0 = max(kb, -1)
                            t1 = min(kb + 4, 3)
                            if t1 - t0 + 1 == 5:
                                csteps.append((p4, kb, t0, t0 + 3, False))
                                csteps.append((p4, kb, t1, t1, False))
                            else:
                                csteps.append((p4, kb, t0, t1, kb == 3))

                    def c_qk(si):
                        p4, kb, t0, t1, last = csteps[si]
                        blk = kb + 5
                        n = (t1 - t0 + 1) * 128
                        sidx = si % 3
                        qc0 = 128 * (t0 + 1)
                        for hh in range(2):
                            mm(PS[:, 2 * sidx + hh, 0:n], KbT[:, p4, blk * 128:(blk + 1) * 128],
                               QbT[:, p4, hh, qc0:qc0 + n], True, True, [b_KbT, b_QbT], [BK[2 * sidx + hh]])

                    def c_rest(si):
                        p4, kb, t0, t1, last = csteps[si]
                        blk = kb + 5
                        n = (t1 - t0 + 1) * 128
                        sidx = si % 3
                        dl0 = t0 - kb
                        S = PS[:, 2 * sidx:2 * sidx + 2, 0:n]
                        bS = [BK[2 * sidx], BK[2 * sidx + 1]]
                        if blk <= 4:
                            act(PTb[sidx][:, :, 0:n], S, AF.Exp, bS + [b_cst], [b_PTb[sidx]], bias=cc(C_WVB + i), scale=0.125)
                        else:
                            act(PTb[sidx][:, :, 0:n], S, AF.Exp, bS, [b_PTb[sidx]], scale=0.125)
                        tt("dve", PTc[sidx][:, :, 0:n], PTb[sidx][:, :, 0:n], EB3[:, 2 * p4:2 * p4 + 2, dl0 * 128:dl0 * 128 + n], ALU.mult,
                           [b_PTb[sidx], b_EB], [b_PTc[sidx]])
                        for hh in range(2):
                            h = 2 * p4 + hh
                            ob = 6 + hh
                            OB = PS[:, ob, 0:325].rearrange("p (a b) -> p a b", b=65)
                            for j in range(t1 - t0 + 1):
                                t = t0 + j
                                mm(OB[:, t + 1, :], PTc[sidx][:, hh, j * 128:(j + 1) * 128], Vb[:, blk, h, :], kb == -5 and j == 0, kb == t,
                                   [b_PTc[sidx], b_Vb], [BK[ob]])
                            if last:
                                r0 = 64 * hh
                                ts("dve", L5[:, hh, :], OB[:, :, 64], 1e-30, None, ALU.max, None, [BK[ob]], [bL5[hh]])
                                P.op("dve", lambda e, hh=hh: e.reciprocal(out=R5[:, hh, :], in_=L5[:, hh, :]), [bL5[hh]], [bR5[hh]])
                                for t in range(-1, 4):
                                    ts("dve", OBn[p4][:, t + 1, r0:r0 + 64], OB[:, t + 1, 0:64], R5[:, hh, t + 1:t + 2], None, ALU.mult, None,
                                       [BK[ob], bR5[hh]], [bOBn[p4]])

                    def c_tr(p4):
                        for t in range(-1, 4):
                            tb_ = cnt_t[0] % 4; cnt_t[0] += 1
                            pso = PS[:, tb_, 0:64].bitcast(BF16)
                            tr(pso, OBn[p4][:, t + 1, :], identb[:], [bOBn[p4], b_identb], [BK[tb_]])
                            if t >= 0:
                                cp("act", oT[:, 8 + p4, i * TC + 2 + 128 * t: i * TC + 2 + 128 * (t + 1)], pso,
                                   [BK[tb_]], [b_oT[8 + p4][i]])
                            else:
                                cp("act", oT[:, 8 + p4, i * TC: i * TC + 2], pso[:, 126:128], [BK[tb_]], [b_oT[8 + p4][i]])

                    c_qk(0)
                    c_qk(1)
                    for si in range(len(csteps)):
                        if si + 2 < len(csteps):
                            c_qk(si + 2)
                        c_rest(si)
                    for p4 in range(4):
                        c_tr(p4)
                P.emit_phase()

            with ExitStack() as scopeT1:
                Wg = sb(scopeT1, [128, 8, 2048], BF16); b_Wg = P.buf()
                Wpa = sb(scopeT1, [128, 8, 1024], BF16); b_Wpa = P.buf()
                Wpb = sb(scopeT1, [128, 4, 1024], BF16); b_Wpb = P.buf()
                Wo = sb(scopeT1, [128, 8, 1024], BF16); b_Wo = P.buf()
                LNP = sb(scopeT1, [128, 2, 1024], F32); b_LNP = P.buf()
                XG = sb(scopeT1, [128, 8, TC], BF16); b_XG = P.buf()
                GG = sb(scopeT1, [128, 2, TC], F32); b_GG = P.buf()
                T1t = sb(scopeT1, [128, TC], F32); b_T1 = P.buf()
                T2t = sb(scopeT1, [128, TC], F32); b_T2 = P.buf()
                MT = sb(scopeT1, [128, 8, TC], BF16); b_MT = [P.buf() for _ in range(8)]
                XTK = [sb(scopeT1, [128, 1024], F32) for _ in range(3)]; b_XTK = [P.buf() for _ in range(3)]
                RR = [sb(scopeT1, [128, 1024], F32) for _ in range(3)]; b_RR = [P.buf() for _ in range(3)]
                H1 = [sb(scopeT1, [128, 1024], F32) for _ in range(3)]; b_H1 = [P.buf() for _ in range(3)]
                H1Ts = sb(scopeT1, [128, 8, TC], BF16); b_H1Ts = P.buf()
                H1b = sb(scopeT1, [128, 1024], BF16); b_H1b = P.buf()
                ST = sb(scopeT1, [128, 2, 6], F32); b_ST = P.buf()
                MV = sb(scopeT1, [128, 2], F32); b_MV = P.buf()
                RSD = sb(scopeT1, [128, 1], F32); b_RSD = P.buf()
                dma("sp", Wg[:], winb_d[:, 4608:6656].rearrange("(k p) n -> p k n", p=128), [], [b_Wg])
                dma("sp", Wpa[:], wpab_d.rearrange("(k p) n -> p k n", p=128), [], [b_Wpa])
                dma("sp", Wpb[:], wpbb_d.rearrange("(k p) n -> p k n", p=128), [], [b_Wpb])
                dma("sp", Wo[:], woutb_d.rearrange("(k p) n -> p k n", p=128), [], [b_Wo])
                for j in range(2):
                    dma("sp", LNP[:, j, :], bc(rows_d[j:j + 1, :]), [], [b_LNP])
                b_H1d = P.buf(); b_H1Td = P.buf()
                HALF = ((0, 257), (257, 257))

                def fm_group(bk0, lhs_list, rhs_fn, R):
                    nk = len(lhs_list)
                    for hf, (c0, n) in enumerate(HALF):
                        for k in range(nk):
                            mm(PS[:, bk0 + hf, 0:n], lhs_list[k], rhs_fn(k, c0, n), k == 0, k == nk - 1, R, [BK[bk0 + hf]])
                    return PS[:, bk0:bk0 + 2, 0:257]

                def v3(t2d):
                    return t2d.rearrange("p (a b) -> p a b", a=2)

                for i in range(NSLOT):
                    own0 = (16 * i + 12) * 128
                    hal0 = NLOC + WIN * i + WIN - 2
                    dma("sp", XG[:, :, 2:TC], XB_d[:, :, own0:own0 + 512], [], [b_XG])
                    dma("sp", XG[:, :, 0:2], XB_d[:, :, hal0:hal0 + 2], [], [b_XG])
                    for m in range(8):
                        for j in range(2):
                            gb = 6 * j
                            g = fm_group(gb, [Wg[:, k, (8 * j + m) * 128:(8 * j + m + 1) * 128] for k in range(8)],
                                         lambda k, c0, n: XG[:, k, c0:c0 + n], [b_Wg, b_XG])
                            act(v3(GG[:, j, :]), g, AF.Sigmoid, [BK[gb], BK[gb + 1], b_cst], [b_GG], bias=cc(C_BG + 8 * j + m), scale=1.0)
                        ya = fm_group(2, [Wpa[:, k, m * 128:(m + 1) * 128] for k in range(8)],
                                      lambda k, c0, n: oT[:, k, i * TC + c0: i * TC + c0 + n], [b_Wpa] + [b_oT[k][i] for k in range(8)])
                        tt("dve", v3(T1t[:]), ya, v3(GG[:, 0, :]), ALU.mult, [BK[2], BK[3], b_GG], [b_T1])
                        yb = fm_group(4, [Wpb[:, k, m * 128:(m + 1) * 128] for k in range(4)],
                                      lambda k, c0, n: oT[:, 8 + k, i * TC + c0: i * TC + c0 + n], [b_Wpb] + [b_oT[8 + k][i] for k in range(4)])
                        tt("dve", v3(T2t[:]), yb, v3(GG[:, 1, :]), ALU.mult, [BK[4], BK[5], b_GG], [b_T2])
                        tt("pool", MT[:, m, :], T1t[:], T2t[:], ALU.add, [b_T1, b_T2], [b_MT[m]])
                    def t_geom(tl):
                        nt = 2 if tl == 0 else 128
                        c0 = 0 if tl == 0 else 2 + 128 * (tl - 1)
                        return nt, c0, i * TC + c0

                    def t_mix(tl):
                        nt, c0, r0 = t_geom(tl)
                        pb = (6, 4, 2)[tl % 3]
                        dma("sp", XTK[tl % 3][:nt, :], xtok_d[r0:r0 + nt, :], [], [b_XTK[tl % 3]])
                        for cg in range(2):
                            for k in range(8):
                                mm(PS[:nt, pb + cg, :], MT[:, k, c0:c0 + nt], Wo[:, k, cg * 512:(cg + 1) * 512], k == 0, k == 7,
                                   [b_MT[k], b_Wo], [BK[pb + cg]])

                    def t_ln(tl):
                        nt, c0, r0 = t_geom(tl)
                        pb = (6, 4, 2)[tl % 3]
                        RRt, bRRt = RR[tl % 3], b_RR[tl % 3]
                        H1t, bH1t = H1[tl % 3], b_H1[tl % 3]
                        stt(RRt[:nt, :], XTK[tl % 3][:nt, :], ALPHA, PS[:nt, pb:pb + 2, :].rearrange("p a b -> p (a b)"), ALU.mult, ALU.add,
                            [b_XTK[tl % 3], BK[pb], BK[pb + 1]], [bRRt])
                        for cg in range(2):
                            P.op("dve", lambda e, cg=cg, nt=nt, RRt=RRt: e.bn_stats(out=ST[:nt, cg, :], in_=RRt[:nt, cg * 512:(cg + 1) * 512]),
                                 [bRRt], [b_ST])
                        P.op("dve", lambda e, nt=nt: e.bn_aggr(out=MV[:nt, :], in_=ST[:nt, :, :].rearrange("p a b -> p (a b)")),
                             [b_ST], [b_MV])
                        act(RSD[:nt, :], MV[:nt, 1:2], AF.Ln, [b_MV, b_cst], [b_RSD], bias=cst[:nt, C_EPS:C_EPS + 1], scale=1.0)
                        act(RSD[:nt, :], RSD[:nt, :], AF.Exp, [b_RSD], [b_RSD], scale=-0.5)
                        ts("dve", RRt[:nt, :], RRt[:nt, :], MV[:nt, 0:1], RSD[:nt, 0:1], ALU.subtract, ALU.mult, [bRRt, b_MV, b_RSD], [bRRt])
                        tt("pool", RRt[:nt, :], RRt[:nt, :], LNP[:nt, 0, :], ALU.mult, [bRRt, b_LNP], [bRRt])
                        tt("pool", H1t[:nt, :], RRt[:nt, :], LNP[:nt, 1, :], ALU.add, [bRRt, b_LNP], [bH1t])
                        dma("pool", H1_d[r0:r0 + nt, :], H1t[:nt, :], [bH1t], [b_H1d])

                    def t_tr(tl):
                        nt, c0, r0 = t_geom(tl)
                        H1t, bH1t = H1[tl % 3], b_H1[tl % 3]
                        cp("dve", H1b[:nt, :], H1t[:nt, :], [bH1t], [b_H1b])
                        for g in range(2):
                            psb = PS[:, g, :].bitcast(BF16).rearrange("p (a b) -> p a b", b=256)
                            for j in range(4):
                                k = 4 * g + j
                                tr(psb[:, j, 0:nt], H1b[:nt, k * 128:(k + 1) * 128], identb[:nt, :nt], [b_H1b, b_identb], [BK[g]])
                            cp("act", H1Ts[:, 4 * g:4 * g + 4, c0:c0 + nt], psb[:, :, 0:nt], [BK[g]], [b_H1Ts])

                    t_mix(0)
                    t_mix(1)
                    for tl in range(5):
                        if tl + 2 < 5:
                            t_mix(tl + 2)
                        t_ln(tl)
                        t_tr(tl)
                    dma("act", H1T_d[:, :, i * TC:(i + 1) * TC], H1Ts[:], [b_H1Ts], [b_H1Td])
                P.emit_phase()

        with ExitStack() as scopeT2:
            Wup = sb(scopeT2, [128, 8, 2 * DFF], BF16); b_Wup = P.buf()
            Wdn = sb(scopeT2, [128, 22, 1024], BF16); b_Wdn = P.buf()
            LNP = sb(scopeT2, [128, 2, 1024], F32); b_LNP = P.buf()
            HT = [sb(scopeT2, [128, 8, TC], BF16) for _ in range(2)]; b_HT = [P.buf(), P.buf()]
            UG = sb(scopeT2, [128, TC], F32); b_UG = P.buf()
            UV = sb(scopeT2, [128, TC], F32); b_UV = P.buf()
            CG = sb(scopeT2, [128, TC], F32); b_CG = P.buf()
            CV = sb(scopeT2, [128, TC], F32); b_CV = P.buf()
            SG = sb(scopeT2, [128, 512], F32); b_SG = P.buf()
            AT = sb(scopeT2, [128, 22, 512], BF16); b_AT = [P.buf() for _ in range(22)]
            H1 = sb(scopeT2, [128, 1024], F32); b_H1 = P.buf()
            RR = sb(scopeT2, [128, 1024], F32); b_RR = P.buf()
            OU = sb(scopeT2, [128, 1024], F32); b_OU = P.buf()
            ST = sb(scopeT2, [128, 2, 6], F32); b_ST = P.buf()
            MV = sb(scopeT2, [128, 2], F32); b_MV = P.buf()
            RSD = sb(scopeT2, [128, 1], F32); b_RSD = P.buf()
            b_out = P.buf()
            b_WupP = [[P.buf() for _ in range(4)] for _ in range(2)]
            PIECES = ((0, 6), (6, 12), (12, 17), (17, 22))
            for pc, (ma, mb) in enumerate(PIECES):
                for gv in range(2):
                    ca, cb = (gv * 22 + ma) * 128, (gv * 22 + mb) * 128
                    dma("sp", Wup[:, :, ca:cb], wupb_d[:, ca:cb].rearrange("(k p) n -> p k n", p=128), [], [b_WupP[gv][pc]])

            def wup_buf(gv, m):
                for pc, (ma, mb) in enumerate(PIECES):
                    if ma <= m < mb:
                        return b_WupP[gv][pc]
            dma("sp", Wdn[:], wdnb_d.rearrange("(k p) n -> p k n", p=128), [], [b_Wdn])
            for j in range(2):
                dma("sp", LNP[:, j, :], bc(rows_d[2 + j:3 + j, :]), [], [b_LNP])
            HALF = ((0, 257), (257, 257))
            A0G = CG; b_A0G = b_CG
            A0V = CV; b_A0V = b_CV
            dma("sp", HT[0][:], H1T_d[:, :, 0:TC], [], [b_HT[0]])
            for i in range(NSLOT):
                HTi = HT[i % 2]; bHTi = b_HT[i % 2]
                for m in range(22):
                    for (gv, U, bU, A0, bA0, bk0) in ((0, UG, b_UG, A0G, b_A0G, 0), (1, UV, b_UV, A0V, b_A0V, 2)):
                        mt = gv * 22 + m
                        for hf, (c0, n) in enumerate(HALF):
                            for k in range(8):
                                mm(PS[:, bk0 + hf, 0:n], Wup[:, k, mt * 128:(mt + 1) * 128], HTi[:, k, c0:c0 + n], k == 0, k == 7,
                                   [wup_buf(gv, m), bHTi], [BK[bk0 + hf]])
                        act(U[:].rearrange("p (a b) -> p a b", a=2), PS[:, bk0:bk0 + 2, 0:257], AF.Copy, [BK[bk0], BK[bk0 + 1]], [bU])
                        act(A0[:].rearrange("p (a b) -> p a b", a=2), PS[:, bk0:bk0 + 2, 0:257], AF.Identity,
                            [BK[bk0], BK[bk0 + 1], b_cst], [bA0], bias=cc(C_CB + mt), scale=cc(C_CW + 2 * 44 + mt))
                        if i == 0:
                            ts("dve", U[:, 0:2], U[:, 0:2], cc(C_HV + i), None, ALU.mult, None, [bU, b_cst], [bU])
                        stt(A0[:, 2:TC], U[:, 1:TC - 1], cc(C_CW + 44 + mt), A0[:, 2:TC], ALU.mult, ALU.add, [bU, bA0, b_cst], [bA0])
                        stt(A0[:, 2:TC], U[:, 0:TC - 2], cc(C_CW + mt), A0[:, 2:TC], ALU.mult, ALU.add, [bU, bA0, b_cst], [bA0])
                    act(SG[:], A0G[:, 2:TC], AF.Silu, [b_A0G], [b_SG])
                    tt("pool", AT[:, m, :], SG[:], A0V[:, 2:TC], ALU.mult, [b_SG, b_A0V], [b_AT[m]])
                if i + 1 < NSLOT:
                    dma("sp", HT[(i + 1) % 2][:], H1T_d[:, :, (i + 1) * TC:(i + 2) * TC], [], [b_HT[(i + 1) % 2]])
                for tl in range(4):
                    r0 = i * TC + 2 + 128 * tl
                    pb = 4 + 2 * (tl % 2)
                    dma("sp", H1[:], H1_d[r0:r0 + 128, :], [], [b_H1])
                    for cg in range(2):
                        for k in range(22):
                            mm(PS[:, pb + cg, :], AT[:, k, tl * 128:(tl + 1) * 128], Wdn[:, k, cg * 512:(cg + 1) * 512], k == 0, k == 21,
                               [b_AT[k], b_Wdn], [BK[pb + cg]])
                    stt(RR[:], H1[:], ALPHA, PS[:, pb:pb + 2, :].rearrange("p a b -> p (a b)"), ALU.mult, ALU.add, [b_H1, BK[pb], BK[pb + 1]], [b_RR])
                    for cg in range(2):
                        P.op("dve", lambda e, cg=cg: e.bn_stats(out=ST[:, cg, :], in_=RR[:, cg * 512:(cg + 1) * 512]), [b_RR], [b_ST])
                    P.op("dve", lambda e: e.bn_aggr(out=MV[:], in_=ST[:].rearrange("p a b -> p (a b)")), [b_ST], [b_MV])
                    act(RSD[:], MV[:, 1:2], AF.Ln, [b_MV, b_cst], [b_RSD], bias=cc(C_EPS), scale=1.0)
                    act(RSD[:], RSD[:], AF.Exp, [b_RSD], [b_RSD], scale=-0.5)
                    ts("dve", RR[:], RR[:], MV[:, 0:1], RSD[:, 0:1], ALU.subtract, ALU.mult, [b_RR, b_MV, b_RSD], [b_RR])
                    tt("pool", RR[:], RR[:], LNP[:, 0, :], ALU.mult, [b_RR, b_LNP], [b_RR])
                    tt("pool", OU[:], RR[:], LNP[:, 1, :], ALU.add, [b_RR, b_LNP], [b_OU])
                    dma("pool", out_d[i * 512 + tl * 128: i * 512 + (tl + 1) * 128, :], OU[:], [b_OU], [b_out])
            P.emit_phase()
        P.finish()
    return nc


_PROG = [None]


def _host_inputs(inputs):
    x = np.asarray(inputs["x"], np.float32)
    pos = np.asarray(inputs["positions"], np.int32)
    w_in = np.asarray(inputs["w_in"], np.float32)[0]
    perm = np.concatenate([np.arange(0, 32), np.arange(64, 96), np.arange(32, 64), np.arange(96, 128)])
    cols = np.arange(6656)
    for base in (0, 1024):
        for h in range(8):
            cols[base + 128 * h: base + 128 * (h + 1)] = base + 128 * h + perm
    w_in_p = np.ascontiguousarray(w_in[:, cols])
    rel = np.asarray(inputs["rel_bias"], np.float32)[0]
    k_ = np.arange(128)[:, None]
    q_ = np.arange(128)[None, :]
    bt = np.zeros((128, 40 * 128), np.float32)
    for h in range(8):
        for dl in range(5):
            idx = np.clip((k_ - q_) - 128 * dl, -256, 63) + 256
            bt[:, (h * 5 + dl) * 128:(h * 5 + dl + 1) * 128] = rel[h][idx]
    rows = np.zeros((9, D), np.float32)
    rows[0] = inputs["ln1_g"][0]; rows[1] = inputs["ln1_b"][0]
    rows[2] = inputs["ln2_g"][0]; rows[3] = inputs["ln2_b"][0]
    rows[4, :128] = inputs["subln_g"][0]
    rows[5, :64] = inputs["lambda_q1"][0]; rows[6, :64] = inputs["lambda_k1"][0]
    rows[7, :64] = inputs["lambda_q2"][0]; rows[8, :64] = inputs["lambda_k2"][0]
    ident = np.eye(128, dtype=np.float32)
    masks = np.zeros((128, 256), np.float32)
    masks[64:128, 0:64] = NEG
    masks[0:64, 128 + 64:256] = NEG
    invf = (np.float32(1.0) / (np.float32(10000.0) ** (np.arange(0, 64, 2, dtype=np.float32) / np.float32(64)))).astype(np.float32)
    p = np.arange(128)
    cst0 = np.zeros((128, NCST), np.float32)
    cst0[:, C_INVF] = invf[p % 32]
    sgn = np.where(p < 64, 1.0, -1.0).astype(np.float32)
    cst0[:, C_SGN] = sgn
    cst0[:, C_HPI] = np.float32(PI / 2)
    cst0[:, C_EPS] = np.float32(EPS)
    cst0[:, C_BG:C_BG + 16] = np.asarray(inputs["b_gate"], np.float32)[0].reshape(16, 128).T
    cw = np.asarray(inputs["conv_w"], np.float32)[0]
    for j in range(3):
        cst0[:, C_CW + 44 * j: C_CW + 44 * (j + 1)] = cw[j].reshape(44, 128).T
    cst0[:, C_CB:C_CB + 44] = np.asarray(inputs["conv_b"], np.float32)[0].reshape(44, 128).T
    shared = {
        "w_in": w_in_p, "w_pa": np.ascontiguousarray(inputs["w_proj_a"][0], np.float32),
        "w_pb": np.ascontiguousarray(inputs["w_proj_b"][0], np.float32),
        "w_out": np.ascontiguousarray(inputs["w_out"][0], np.float32),
        "w_up": np.ascontiguousarray(inputs["w_up"][0], np.float32),
        "w_dn": np.ascontiguousarray(inputs["w_down"][0], np.float32),
        "bt": bt, "rows": rows, "ident": ident, "masks": masks,
    }
    maps = []
    for c in range(8):
        b, r = c // 4, c % 4
        tok = []
        for g in range(4):
            for j in range(4):
                if j != r:
                    tok.append(np.arange(512 * (4 * g + j), 512 * (4 * g + j + 1)))
            tok.append(np.arange(512 * (4 * g + r), 512 * (4 * g + r + 1)))
        for i in range(4):
            t0 = 512 * (4 * i + r)
            tok.append(np.arange(t0 - WIN, t0))
        tok = np.concatenate(tok)
        valid = tok >= 0
        tokc = np.where(valid, tok, 0)
        xT = np.ascontiguousarray(x[b].T[:, tokc])
        xT[:, ~valid] = 0.0
        ps = np.where(valid, pos[b][tokc], 0).astype(np.int32)[None, :]
        xtok = np.zeros((NSLOT * TC, D), np.float32)
        cst = cst0.copy()
        for o in range(3):
            cst[:, C_VB + o] = 0.0 if o < r else NEG
        for i in range(4):
            t0 = 512 * (4 * i + r)
            xtok[i * TC + 2:(i + 1) * TC] = x[b, t0:t0 + 512]
            if t0 >= 2:
                xtok[i * TC:i * TC + 2] = x[b, t0 - 2:t0]
            cst[:, C_WVB + i] = 0.0 if t0 > 0 else NEG
            cst[:, C_HV + i] = 1.0 if t0 > 0 else 0.0
        m = dict(shared)
        m.update({"xT": xT, "pos": ps, "xtok": xtok, "cst": cst})
        maps.append(m)
    return maps


def kernel(**inputs):
    if _PROG[0] is None:
        _PROG[0] = build_program()
    nc = _PROG[0]
    maps = _host_inputs(inputs)
    res = run_bass_kernel_spmd(nc, maps, core_ids=list(range(8)))
    out = np.zeros((2, SEQ, D), np.float32)
    for c in range(8):
        b, r = c // 4, c % 4
        o = np.asarray(res.results[c]["out"], np.float32)
        for i in range(4):
            s = 4 * i + r
            out[b, 512 * s:512 * (s + 1)] = o[512 * i:512 * (i + 1)]
    return out
```
